# Optimizing a Trainium2 kernel written in Bass

```python
import math
import jax, jax.numpy as jnp
from jax import lax
import numpy as np

D_MODEL = 1024
BATCH = 16
SEQ = 4096
DEPTH = 4

N_MIXERS = 2
N_S5_LAYERS = (DEPTH + 1) // 2
N_ATTN_LAYERS = DEPTH // 2
S5_GROUP = 16
S5_GROUPS = D_MODEL // S5_GROUP
S5_STATE = 64
S5_DT_MIN = 1e-3
S5_DT_MAX = 1e-1
S5_EIG_CLIP = -1e-4
ATTN_HEAD_DIM = 64
ATTN_HEADS = D_MODEL // (2 * ATTN_HEAD_DIM)
Q_BLOCK = 128
REL_BUCKETS = 32
REL_MAX_DIST = 128
D_FF = 2816
CONV_WIDTH = 3
NORM_EPS = 1e-6
SUBLN_EPS = 1e-5

kernel_name = "hybrid_s5_diffattn_convffn_encoder"


def rmsnorm(x, g, eps=NORM_EPS):
    xf = x.astype(jnp.float32)
    xf = xf * lax.rsqrt(jnp.mean(xf * xf, axis=-1, keepdims=True) + eps)
    return (xf * g.astype(jnp.float32)).astype(x.dtype)


def rel_bucket(rel):
    nb = REL_BUCKETS // 2
    ret = jnp.where(rel > 0, nb, 0)
    n = jnp.abs(rel)
    max_exact = nb // 2
    nf = jnp.maximum(n, 1).astype(jnp.float32)
    large = max_exact + (jnp.log(nf / max_exact) / math.log(REL_MAX_DIST / max_exact)
                         * (nb - max_exact)).astype(jnp.int32)
    large = jnp.minimum(large, nb - 1)
    return ret + jnp.where(n < max_exact, n, large)


def _ssm_combine(e1, e2):
    a1r, a1i, b1r, b1i = e1
    a2r, a2i, b2r, b2i = e2
    ar = a1r * a2r - a1i * a2i
    ai = a1r * a2i + a1i * a2r
    br = a2r * b1r - a2i * b1i + b2r
    bi = a2r * b1i + a2i * b1r + b2i
    return (ar, ai, br, bi)


def s5_mixer(h, lam_re, lam_im, log_step, b_re, b_im, c_re, c_im, d_skip, glu_w, glu_b):
    bsz, seq, dm = h.shape
    u = h.reshape(bsz, seq, S5_GROUPS, S5_GROUP)
    y = (u * d_skip.reshape(S5_GROUPS, S5_GROUP)).astype(jnp.float32)
    for direction in range(2):
        lre = jnp.minimum(lam_re[direction], S5_EIG_CLIP)
        lim = lam_im[direction]
        step = jnp.exp(log_step[direction])[:, None]
        mag = jnp.exp(step * lre)
        ang = step * lim
        abar_re = mag * jnp.cos(ang)
        abar_im = mag * jnp.sin(ang)
        nr = abar_re - 1.0
        ni = abar_im
        den = lre * lre + lim * lim
        fr = (nr * lre + ni * lim) / den
        fi = (ni * lre - nr * lim) / den
        br, bi = b_re[direction], b_im[direction]
        bbar_re = fr[..., None] * br - fi[..., None] * bi
        bbar_im = fr[..., None] * bi + fi[..., None] * br
        bu_re = jnp.einsum('blgc,gpc->lbgp', u, bbar_re)
        bu_im = jnp.einsum('blgc,gpc->lbgp', u, bbar_im)
        a_re = jnp.broadcast_to(abar_re[None, None], (seq, 1, S5_GROUPS, S5_STATE))
        a_im = jnp.broadcast_to(abar_im[None, None], (seq, 1, S5_GROUPS, S5_STATE))
        _, _, s_re, s_im = lax.associative_scan(
            _ssm_combine, (a_re, a_im, bu_re, bu_im), reverse=(direction == 1), axis=0)
        y = y + (jnp.einsum('lbgp,gcp->blgc', s_re, c_re[direction])
                 - jnp.einsum('lbgp,gcp->blgc', s_im, c_im[direction])).astype(jnp.float32)
    y = y.reshape(bsz, seq, dm).astype(h.dtype)
    g = jax.nn.gelu(y)
    return g * jax.nn.sigmoid(g @ glu_w + glu_b)


def diff_attention(h, w_qkv, lq1, lk1, lq2, lk2, subln_g, w_o, rel_bias, lambda_init):
    bsz, seq, dm = h.shape
    qkv = h @ w_qkv
    q, k, v = jnp.split(qkv, 3, axis=-1)
    q = q.reshape(bsz, seq, ATTN_HEADS, 2, ATTN_HEAD_DIM) * (ATTN_HEAD_DIM ** -0.5)
    k = k.reshape(bsz, seq, ATTN_HEADS, 2, ATTN_HEAD_DIM)
    v = v.reshape(bsz, seq, ATTN_HEADS, 2 * ATTN_HEAD_DIM)
    lam = (jnp.exp(jnp.sum(lq1.astype(jnp.float32) * lk1.astype(jnp.float32)))
           - jnp.exp(jnp.sum(lq2.astype(jnp.float32) * lk2.astype(jnp.float32)))
           + lambda_init)
    nblk = seq // Q_BLOCK
    qb = q.reshape(bsz, nblk, Q_BLOCK, ATTN_HEADS, 2, ATTN_HEAD_DIM).transpose(1, 0, 2, 3, 4, 5)
    k_pos = jnp.arange(seq, dtype=jnp.int32)

    def block(args):
        qi, bi = args
        q_pos = bi * Q_BLOCK + jnp.arange(Q_BLOCK, dtype=jnp.int32)
        bucket = rel_bucket(k_pos[None, :] - q_pos[:, None])
        bias = rel_bias[bucket].astype(jnp.float32).transpose(2, 0, 1)
        s = jnp.einsum('bqhtd,bkhtd->bhtqk', qi, k).astype(jnp.float32) + bias[None, :, None]
        p = jax.nn.softmax(s, axis=-1)
        pd = (p[:, :, 0] - lam * p[:, :, 1]).astype(v.dtype)
        return jnp.einsum('bhqk,bkhe->bqhe', pd, v)

    out = lax.map(block, (qb, jnp.arange(nblk, dtype=jnp.int32)))
    out = out.transpose(1, 0, 2, 3, 4).reshape(bsz, seq, ATTN_HEADS, 2 * ATTN_HEAD_DIM)
    out = rmsnorm(out, subln_g, SUBLN_EPS) * (1.0 - lambda_init)
    return out.reshape(bsz, seq, dm) @ w_o


def conv_ffn(h, w_up, conv_w, conv_b, w_down):
    u = h @ w_up
    ch = u.shape[-1]
    u = lax.conv_general_dilated(
        u, conv_w[:, None, :].astype(u.dtype), window_strides=(1,),
        padding=[((CONV_WIDTH - 1) // 2, (CONV_WIDTH - 1) // 2)],
        dimension_numbers=('NWC', 'WIO', 'NWC'), feature_group_count=ch) + conv_b
    g, val = jnp.split(u, 2, axis=-1)
    return (jax.nn.silu(g) * val) @ w_down


def setup_inputs(seed: int = 0) -> dict:
    key = jax.random.key(seed)
    ks = jax.random.split(key, 32)
    nA, nB = N_S5_LAYERS, N_ATTN_LAYERS
    G, P, C = S5_GROUPS, S5_STATE, S5_GROUP
    D, F, dh = D_MODEL, D_FF, ATTN_HEAD_DIM
    nrm = jax.random.normal
    f32 = jnp.float32
    lam_im_base = jnp.pi * jnp.arange(P, dtype=f32)
    return {
        "x": nrm(ks[0], (BATCH, SEQ, D), f32),
        "rel_bias": 0.5 * nrm(ks[1], (REL_BUCKETS, ATTN_HEADS), f32),
        "norm_mix_g": 1.0 + 0.02 * nrm(ks[2], (DEPTH, D), f32),
        "norm_ffn_g": 1.0 + 0.02 * nrm(ks[3], (DEPTH, D), f32),
        "final_norm_g": 1.0 + 0.02 * nrm(ks[4], (D,), f32),
        "s5_lambda_re": -0.5 + 0.01 * nrm(ks[5], (nA, 2, G, P), f32),
        "s5_lambda_im": lam_im_base + 0.01 * nrm(ks[6], (nA, 2, G, P), f32),
        "s5_log_step": jax.random.uniform(ks[7], (nA, 2, G), f32,
                                          minval=math.log(S5_DT_MIN), maxval=math.log(S5_DT_MAX)),
        "s5_b_re": nrm(ks[8], (nA, 2, G, P, C), f32) * (0.5 / C) ** 0.5,
        "s5_b_im": nrm(ks[9], (nA, 2, G, P, C), f32) * (0.5 / C) ** 0.5,
        "s5_c_re": nrm(ks[10], (nA, 2, G, C, P), f32) * (0.5 / P) ** 0.5,
        "s5_c_im": nrm(ks[11], (nA, 2, G, C, P), f32) * (0.5 / P) ** 0.5,
        "s5_d": nrm(ks[12], (nA, D), f32),
        "s5_glu_w": nrm(ks[13], (nA, D, D), f32) * D ** -0.5,
        "s5_glu_b": 0.01 * nrm(ks[14], (nA, D), f32),
        "attn_w_qkv": nrm(ks[15], (nB, D, 3 * D), f32) * D ** -0.5,
        "attn_lambda_q1": 0.1 * nrm(ks[16], (nB, dh), f32),
        "attn_lambda_k1": 0.1 * nrm(ks[17], (nB, dh), f32),
        "attn_lambda_q2": 0.1 * nrm(ks[18], (nB, dh), f32),
        "attn_lambda_k2": 0.1 * nrm(ks[19], (nB, dh), f32),
        "attn_subln_g": 1.0 + 0.02 * nrm(ks[20], (nB, 2 * dh), f32),
        "attn_w_o": nrm(ks[21], (nB, D, D), f32) * D ** -0.5,
        "ffn_w_up": nrm(ks[22], (DEPTH, D, 2 * F), f32) * D ** -0.5,
        "ffn_conv_w": nrm(ks[23], (DEPTH, CONV_WIDTH, 2 * F), f32) * CONV_WIDTH ** -0.5,
        "ffn_conv_b": 0.01 * nrm(ks[24], (DEPTH, 2 * F), f32),
        "ffn_w_down": nrm(ks[25], (DEPTH, F, D), f32) * F ** -0.5,
    }


def reference(x, rel_bias, norm_mix_g, norm_ffn_g, final_norm_g,
              s5_lambda_re, s5_lambda_im, s5_log_step, s5_b_re, s5_b_im, s5_c_re, s5_c_im,
              s5_d, s5_glu_w, s5_glu_b,
              attn_w_qkv, attn_lambda_q1, attn_lambda_k1, attn_lambda_q2, attn_lambda_k2,
              attn_subln_g, attn_w_o,
              ffn_w_up, ffn_conv_w, ffn_conv_b, ffn_w_down):
    h = x
    for i in range(DEPTH):
        hn = rmsnorm(h, norm_mix_g[i])
        j = i // N_MIXERS
        if i % N_MIXERS == 0:
            h = h + s5_mixer(hn, s5_lambda_re[j], s5_lambda_im[j], s5_log_step[j],
                             s5_b_re[j], s5_b_im[j], s5_c_re[j], s5_c_im[j],
                             s5_d[j], s5_glu_w[j], s5_glu_b[j])
        else:
            lambda_init = 0.8 - 0.6 * math.exp(-0.3 * i)
            h = h + diff_attention(hn, attn_w_qkv[j], attn_lambda_q1[j], attn_lambda_k1[j],
                                   attn_lambda_q2[j], attn_lambda_k2[j], attn_subln_g[j],
                                   attn_w_o[j], rel_bias, lambda_init)
        h = h + conv_ffn(rmsnorm(h, norm_ffn_g[i]), ffn_w_up[i], ffn_conv_w[i],
                         ffn_conv_b[i], ffn_w_down[i])
    return rmsnorm(h, final_norm_g)
```

```python
import math
from contextlib import ExitStack
import numpy as np
import ml_dtypes
import concourse.bass as bass
import concourse.mybir as mybir
from concourse.bass_utils import run_bass_kernel_spmd

F32 = mybir.dt.float32
BF16 = mybir.dt.bfloat16
I32 = mybir.dt.int32
ALU = mybir.AluOpType
AF = mybir.ActivationFunctionType

D = 1024
NDT = 8
DFF = 2816
NFI = 22
NORM_EPS = 1e-6
SUBLN_EPS = 1e-5
SB_BASE = 16640
SB_END = 229376

SAME_ENGINE_SYNC = True
NDMA_SLOTS = 6


class Op:
    __slots__ = ('eng', 'fn', 'deps', 'dma', 'need_inc', 'sem', 'val', 'prev_slot', 'bg')


class Sched:
    ENGS = ['pe', 'act', 'dve', 'pool', 'sp']
    BENGS = ['pe', 'act', 'dve', 'sp']

    def __init__(self):
        self.q = {e: [] for e in self.ENGS}
        self.st = {}
        self.nops = 0

    def add(self, eng, fn, r=(), w=(), dma=False, bg=False):
        o = Op()
        o.bg = bg
        o.eng = eng; o.fn = fn; o.dma = dma; o.need_inc = False
        o.sem = None; o.val = 0; o.prev_slot = None
        deps = set()
        st = self.st
        for k in r:
            s = st.get(k)
            if s is not None and s[0] is not None:
                deps.add(s[0])
        for k in w:
            s = st.get(k)
            if s is not None:
                if s[0] is not None:
                    deps.add(s[0])
                deps.update(s[1])
        for k in r:
            s = st.get(k)
            if s is None:
                st[k] = [None, [o]]
            else:
                s[1].append(o)
        for k in w:
            st[k] = [o, []]
        deps.discard(o)
        o.deps = deps
        self.q[eng].append(o)
        self.nops += 1
        return o

    KEEP = ('wup_s', 'wdn_s', 'cvf', 'cvb', 'wscr', 'wqkv_s', 'wo_s', 'wglu_s')

    def barrier(self):
        lasts = []
        for e in self.ENGS:
            cnt = 0
            gotc = False
            for o in reversed(self.q[e]):
                if o.fn is None and not o.dma:
                    break
                if o.bg:
                    continue
                if o.dma:
                    if cnt < NDMA_SLOTS:
                        lasts.append(o); cnt += 1
                elif not gotc:
                    lasts.append(o); gotc = True
                if gotc and cnt >= NDMA_SLOTS:
                    break
        for e in self.ENGS:
            o = Op()
            o.bg = False
            o.eng = e; o.fn = None; o.dma = False; o.need_inc = False
            o.sem = None; o.val = 0; o.prev_slot = None
            o.deps = set(lasts)
            self.q[e].append(o)
        self.st = {k: v for k, v in self.st.items() if isinstance(k, tuple) and k[0] in self.KEEP}

    def emit(self, nc, stack):
        csem = {}
        for e in ['pe', 'act', 'dve', 'pool']:
            csem[e] = stack.enter_context(nc.semaphore('s_' + e))
        dsem = {}
        for e in self.ENGS:
            if any(o.dma for o in self.q[e]):
                dsem[e] = [stack.enter_context(nc.semaphore('d_%s%d' % (e, i))) for i in range(NDMA_SLOTS)]

        def skip(d, o):
            return (not d.dma) and (not o.dma) and d.eng == o.eng and (d.eng == 'pe' or not SAME_ENGINE_SYNC)

        for e in self.ENGS:
            for o in self.q[e]:
                for d in o.deps:
                    if d.dma or skip(d, o):
                        continue
                    d.need_inc = True
        finals = {}
        for e in self.ENGS:
            cnt = 0
            di = 0
            uses = [0] * NDMA_SLOTS
            lastop = [None] * NDMA_SLOTS
            for o in self.q[e]:
                if o.dma:
                    sl = di % NDMA_SLOTS
                    di += 1
                    uses[sl] += 1
                    o.sem = dsem[e][sl]; o.val = 16 * uses[sl]
                    o.prev_slot = lastop[sl]
                    lastop[sl] = o
                elif o.need_inc:
                    cnt += 1
                    o.sem = csem[e]; o.val = cnt
            finals[e] = [x for x in lastop if x is not None]
        sched = self

        def run(e, eng):
            seen = {}

            def wait(d):
                key = id(d.sem)
                if seen.get(key, 0) >= d.val:
                    return
                seen[key] = d.val
                eng.wait_ge(d.sem, d.val)

            for o in sched.q[e]:
                for d in o.deps:
                    if skip(d, o):
                        continue
                    wait(d)
                if o.dma and o.prev_slot is not None:
                    wait(o.prev_slot)
                if o.fn is None:
                    continue
                ins = o.fn(eng)
                if o.dma:
                    ins.then_inc(o.sem, 16)
                elif o.need_inc:
                    ins.then_inc(o.sem, 1)
            for o in finals[e]:
                wait(o)

        block = stack.enter_context(nc.Block())

        @block.tensor
        def _(eng):
            run('pe', eng)

        @block.scalar
        def _(eng):
            run('act', eng)

        @block.vector
        def _(eng):
            run('dve', eng)

        @block.gpsimd
        def _(eng):
            run('pool', eng)

        @block.sync
        def _(eng):
            run('sp', eng)


class Arena:
    def __init__(self, nc, base, end):
        self.nc = nc; self.base = base; self.end = end; self.off = base; self.n = 0

    def tile(self, name, shape, dtype):
        esz = 2 if str(dtype) == str(BF16) else 4
        nbytes = esz
        for s in shape[1:]:
            nbytes *= s
        off = (self.off + 31) // 32 * 32
        assert off + nbytes <= self.end, ("SBUF overflow", name, off + nbytes, self.end)
        self.n += 1
        t = self.nc.alloc_sbuf_tensor_at("%s_%d_%d" % (name, self.base, self.n), list(shape), dtype, offset=off)
        self.off = off + nbytes
        return t

    def sub(self):
        return Arena(self.nc, (self.off + 31) // 32 * 32, self.end)


class K:
    def __init__(self, cfg):
        self.cfg = cfg
        self.NSEQ = cfg['nseq']
        self.L = cfg['L']
        self.NTOK = self.NSEQ * self.L
        self.layers = cfg['layers']
        self.nc = bass.Bass("TRN2", target_bir_lowering=False)
        self.S = Sched()
        self.dram = {}

    def din(self, name, shape, dtype=F32):
        t = self.nc.dram_tensor(name, list(shape), dtype, kind="ExternalInput")
        self.dram[name] = t
        return t

    def dscr(self, name, shape, dtype):
        t = self.nc.dram_tensor(name, list(shape), dtype)
        self.dram[name] = t
        return t

    def build(self):
        nc, S, cfg = self.nc, self.S, self.cfg
        NTOK = self.NTOK
        depth = cfg['depth']
        nA, nB = cfg['nA'], cfg['nB']
        self.x = self.din("x", [NTOK, D])
        self.out = nc.dram_tensor("out", [NTOK, D], F32, kind="ExternalOutput")
        self.norm_mix_g = self.din("norm_mix_g", [depth, D])
        self.norm_ffn_g = self.din("norm_ffn_g", [depth, D])
        self.final_norm_g = self.din("final_norm_g", [D])
        if cfg['ffn_layers']:
            self.ffn_w_up = self.din("ffn_w_up", [depth, D, 2 * DFF])
            self.ffn_conv_w = self.din("ffn_conv_w", [depth, 3, 2 * DFF])
            self.ffn_conv_b = self.din("ffn_conv_b", [depth, 2 * DFF])
            self.ffn_w_down = self.din("ffn_w_down", [depth, DFF, D])
        self.ident_in = self.din("ident", [128, 128])
        if any(k == 's5' for k, _ in self.layers):
            self.s5_lre = self.din("s5_lambda_re", [nA, 2, 64, 64])
            self.s5_lim = self.din("s5_lambda_im", [nA, 2, 64, 64])
            self.s5_lst = self.din("s5_log_step", [nA, 2, 64])
            self.s5_bre = self.din("s5_b_re", [nA, 2, 64, 64, 16])
            self.s5_bim = self.din("s5_b_im", [nA, 2, 64, 64, 16])
            self.s5_cre = self.din("s5_c_re", [nA, 2, 64, 16, 64])
            self.s5_cim = self.din("s5_c_im", [nA, 2, 64, 16, 64])
            self.s5_d = self.din("s5_d", [nA, D])
            self.s5_glu_w = self.din("s5_glu_w", [nA, D, D])
            self.s5_glu_b = self.din("s5_glu_b", [nA, D])
            self.selA_in = self.din("selA", [128, 8, 8, 128], BF16)
            self.selB_in = self.din("selB", [128, 8, 8, 128], BF16)
            self.maskF_in = self.din("maskF", [128, 128])
            self.maskB_in = self.din("maskB", [128, 128])
            self.wglu_s = [self.dscr("wglu_s%d" % i, [D, D], BF16) for i in range(nA)]
            self.Wtab = self.dscr("Wtab", [128, 2, 64, 2, 64], BF16)
            self.Ftab = self.dscr("Ftab", [64, 2, 64, 2, 128], BF16)
            self.Mtab = self.dscr("Mtab", [128, 64, 128], BF16)
            self.KSd = self.dscr("KSd", [64, 2, 32, 2, 27], F32)
        if any(k == 'attn' for k, _ in self.layers):
            self.attn_w_qkv = self.din("attn_w_qkv", [nB, D, 3 * D])
            self.attn_w_o = self.din("attn_w_o", [nB, D, D])
            self.attn_lq1 = self.din("attn_lambda_q1", [nB, 64])
            self.attn_lk1 = self.din("attn_lambda_k1", [nB, 64])
            self.attn_lq2 = self.din("attn_lambda_q2", [nB, 64])
            self.attn_lk2 = self.din("attn_lambda_k2", [nB, 64])
            self.attn_subln_g = self.din("attn_subln_g", [nB, 128])
            self.reltab = self.din("reltab", [8, 128, 1152])
            self.relfar = self.din("relfar", [128, 16])
            self.wqkv_s = [self.dscr("wqkv_s%d" % i, [D, 3 * D], BF16) for i in range(nB)]
            self.wo_s = [self.dscr("wo_s%d" % i, [D, D], BF16) for i in range(nB)]
            self.QT = self.dscr("QT", [D, NTOK], BF16)
            self.KT = self.dscr("KT", [D, NTOK], BF16)
            self.Vs = self.dscr("Vs", [NTOK, D], BF16)
            self.AT = self.dscr("AT", [D, NTOK], BF16)
        self.hA = self.dscr("hA", [D, NTOK], F32)
        self.hB = self.dscr("hB", [D, NTOK], F32)
        self.hcur, self.hnxt = self.hA, self.hB
        if cfg['ffn_layers']:
            self.wup_s = [self.dscr("wup_s%d" % i, [D, 2 * DFF], BF16) for i in range(depth)]
            self.wdn_s = [self.dscr("wdn_s%d" % i, [NDT, 128, NFI, 128], BF16) for i in range(depth)]

        with ExitStack() as st:
            self.st = st
            self.psall = st.enter_context(nc.psum_tensor("psall", [128, 8, 512], F32))
            self.ps = [self.psall[:, i, :] for i in range(8)]
            top = Arena(nc, SB_BASE, SB_END)
            self.ident = top.tile("ident", [128, 128], F32)
            self.ones_bf = top.tile("ones", [128, 128], BF16)
            self.cv_f = [top.tile("cvf", [128, 1024], F32) for _ in range(2)]
            self.cv_b = [top.tile("cvb", [128, 1024], BF16) for _ in range(2)]
            self.gains = top.tile("gains", [128, 2 * depth + 1, NDT], F32)
            self.top = top
            S.add('sp', lambda e: e.dma_start(out=self.ident[:], in_=self.ident_in.ap()), w=['ident'], dma=True)
            S.add('dve', lambda e: e.memset(self.ones_bf[:], 1.0), w=['ones'])
            S.add('sp', lambda e: e.dma_start(out=self.gains[:, 0:depth, :], in_=self.norm_mix_g.ap().rearrange("l (dt p) -> p l dt", p=128), allow_slow_non_contiguous=True), w=['gains0'], dma=True)
            S.add('sp', lambda e: e.dma_start(out=self.gains[:, depth:2 * depth, :], in_=self.norm_ffn_g.ap().rearrange("l (dt p) -> p l dt", p=128), allow_slow_non_contiguous=True), w=['gains1'], dma=True)
            S.add('sp', lambda e: e.dma_start(out=self.gains[:, 2 * depth, :], in_=self.final_norm_g.ap().rearrange("(dt p) -> p dt", p=128), allow_slow_non_contiguous=True), w=['gains2'], dma=True)
            self.cvn = 0
            for (kind, li) in self.layers:
                if kind == 'ffn':
                    self.convert_ffn_weights(li)
                elif kind == 'attn':
                    self.convert_attn_weights(li // 2)
                elif kind == 's5':
                    self.convert_s5_weights(li // 2)
            self.phase_in()
            for (kind, li) in self.layers:
                if kind == 'ffn':
                    self.phase_ffn(li)
                elif kind == 'attn':
                    self.phase_attn(li)
                elif kind == 's5':
                    self.phase_s5(li)
            self.phase_out()
            S.emit(nc, st)
        return nc

    def convert_ffn_weights(self, li):
        S = self.S
        wup = self.ffn_w_up.ap()[li].rearrange("(dt p) n -> dt p n", p=128)
        dst = self.wup_s[li].ap().rearrange("(dt p) n -> dt p n", p=128)
        for dt in range(NDT):
            for c0 in range(0, 2 * DFF, 1024):
                nc_ = min(1024, 2 * DFF - c0)
                S_src = (lambda dt=dt, c0=c0, nc_=nc_: wup[dt, :, c0:c0 + nc_])
                self._conv_chunk(S_src, (lambda b, dt=dt, c0=c0, nc_=nc_: dst[dt, :, c0:c0 + nc_]), nc_, ('wup_s', li, dt, c0))
        wdn = self.ffn_w_down.ap()[li].rearrange("(fi p) n -> fi p n", p=128)
        dstd = self.wdn_s[li].ap()
        for fi in range(NFI):
            self._conv_chunk((lambda fi=fi: wdn[fi]),
                             (lambda b, fi=fi: dstd[:, :, fi, :].rearrange("dt p c -> p dt c")), 1024, ('wdn_s', li, fi),
                             src_view=(lambda b: b[:, 0:1024].rearrange("p (dt c) -> p dt c", dt=NDT)))

    def _conv_chunk(self, src_fn, dst_fn, ncols, dst_key, src_view=None):
        S = self.S
        i = self.cvn % 2
        self.cvn += 1
        f, b = self.cv_f[i], self.cv_b[i]
        S.add('pool', lambda e: e.dma_start(out=f[:, 0:ncols], in_=src_fn()), w=[('cvf', i)], dma=True, bg=True)
        S.add('pool', lambda e: e.tensor_copy(out=b[:, 0:ncols], in_=f[:, 0:ncols]), r=[('cvf', i)], w=[('cvb', i)], bg=True)
        if src_view is None:
            S.add('pool', lambda e: e.dma_start(out=dst_fn(b), in_=b[:, 0:ncols]), r=[('cvb', i)], w=[dst_key], dma=True, bg=True)
        else:
            S.add('pool', lambda e: e.dma_start(out=dst_fn(b), in_=src_view(b)), r=[('cvb', i)], w=[dst_key], dma=True, bg=True)

    def phase_in(self):
        nc, S = self.nc, self.S
        A = self.top.sub()
        xin = [A.tile("xin", [128, 4, D], F32) for _ in range(2)]
        stg = [A.tile("stg", [128, NDT, 512], F32) for _ in range(2)]
        xv = self.x.ap().rearrange("(b t p) d -> b p t d", p=128, t=4)
        hv = self.hcur.ap().rearrange("(dt p) t -> p dt t", p=128)
        nblk = self.NTOK // 512
        for b in range(nblk):
            i = b % 2
            S.add('sp', lambda e, b=b, i=i: e.dma_start(out=xin[i][:], in_=xv[b]), w=[('xin', i)], dma=True)
            for dt in range(NDT):
                bank = self.ps[dt % 4]
                for t in range(4):
                    S.add('pe', lambda e, i=i, dt=dt, t=t, bank=bank: e.transpose(bank[:, t * 128:(t + 1) * 128], xin[i][:, t, dt * 128:(dt + 1) * 128], self.ident[:]),
                          r=[('xin', i), 'ident'], w=[('psb', dt % 4, t)])
                eng = 'act' if dt % 2 == 0 else 'dve'
                if eng == 'act':
                    S.add('act', lambda e, i=i, dt=dt, bank=bank: e.copy(out=stg[i][:, dt, :], in_=bank[:]),
                          r=[('psb', dt % 4, t) for t in range(4)], w=[('stg', i, dt)])
                else:
                    S.add('dve', lambda e, i=i, dt=dt, bank=bank: e.tensor_copy(out=stg[i][:, dt, :], in_=bank[:]),
                          r=[('psb', dt % 4, t) for t in range(4)], w=[('stg', i, dt)])
            S.add('sp', lambda e, b=b, i=i: e.dma_start(out=hv[:, :, b * 512:(b + 1) * 512], in_=stg[i][:]),
                  r=[('stg', i, dt) for dt in range(NDT)], w=[('h', id(self.hcur), b)], dma=True)
        S.barrier()

    def rmsnorm_fm(self, hT, hT_keys, W2, gain_idx, sq, sq_keys, ps_bank, ps_key, rtmp, rstd, out_fn, out_keys, eps_t, out_eng='dve'):
        S = self.S
        S.add('act', lambda e: e.activation(out=sq[:, :, 0:W2], in_=hT[:, :, 0:W2], func=AF.Square), r=hT_keys, w=sq_keys)
        for dt in range(NDT):
            S.add('pe', lambda e, dt=dt: e.matmul(ps_bank[:, 0:W2], lhsT=self.ones_bf[:], rhs=sq[:, dt, 0:W2], start=(dt == 0), stop=(dt == NDT - 1)),
                  r=[sq_keys[dt], 'ones'], w=[ps_key])
        S.add('act', lambda e: e.activation(out=rtmp[:, 0:W2], in_=ps_bank[:, 0:W2], func=AF.Sqrt, scale=1.0 / D, bias=eps_t[:, 0:1]), r=[ps_key, 'eps'], w=['rtmp'])
        S.add('dve', lambda e: e.reciprocal(out=rstd[:, 0:W2], in_=rtmp[:, 0:W2]), r=['rtmp'], w=['rstd'])
        for dt in range(NDT):
            S.add(out_eng, lambda e, dt=dt: e.scalar_tensor_tensor(out=out_fn(dt), in0=hT[:, dt, 0:W2], scalar=self.gains[:, gain_idx, dt:dt + 1],
                                                                   in1=rstd[:, 0:W2], op0=ALU.mult, op1=ALU.mult),
                  r=[hT_keys[dt], 'rstd', 'gains0', 'gains1', 'gains2'], w=[out_keys[dt]])

    def phase_ffn(self, li):
        nc, S = self.nc, self.S
        depth = self.cfg['depth']
        A = self.top.sub()
        wup = A.tile("wup", [128, NDT, 2 * DFF], BF16)
        wdn = [A.tile("wdn", [128, NFI, 128], BF16) for _ in range(3)]
        hT = [A.tile("hT", [128, NDT, 512], F32) for _ in range(2)]
        hn = A.tile("hn", [128, NDT, 512], BF16)
        hid = A.tile("hid", [128, NFI, 512], BF16)
        sq = hid
        cg = [A.tile("cg", [128, 512], F32) for _ in range(2)]
        cvt = [A.tile("cvt", [128, 512], F32) for _ in range(2)]
        rtmp = A.tile("rtmp", [128, 512], F32)
        rstd = A.tile("rstd", [128, 512], F32)
        cw = A.tile("cw", [128, 3, 2 * NFI], F32)
        cb = A.tile("cb", [128, 2 * NFI], F32)
        eps_t = A.tile("eps", [128, 1], F32)
        S.add('dve', lambda e: e.memset(eps_t[:], NORM_EPS), w=['eps'])
        for k3 in range(3):
            S.add('sp', lambda e, k3=k3: e.dma_start(out=cw[:, k3, :], in_=self.ffn_conv_w.ap()[li, k3].rearrange("(fc p) -> p fc", p=128), allow_slow_non_contiguous=True), w=[('cw', k3)], dma=True)
        S.add('sp', lambda e: e.dma_start(out=cb[:], in_=self.ffn_conv_b.ap()[li].rearrange("(fc p) -> p fc", p=128), allow_slow_non_contiguous=True), w=['cb'], dma=True)
        wsrc = self.wup_s[li].ap().rearrange("(dt p) n -> dt p n", p=128)
        for dt in range(NDT):
            S.add('sp', lambda e, dt=dt: e.dma_start(out=wup[:, dt, :], in_=wsrc[dt]),
                  r=[('wup_s', li, dt, c0) for c0 in range(0, 2 * DFF, 1024)], w=[('wup', dt)], dma=True)
        hsrc = self.hcur.ap().rearrange("(dt p) t -> p dt t", p=128)
        hdst = self.hnxt.ap().rearrange("(dt p) t -> p dt t", p=128)
        wdsrc = self.wdn_s[li].ap()
        L = self.L
        blocks = []
        for s in range(self.NSEQ):
            t0 = 0
            while t0 < L:
                w = min(510, L - t0)
                blocks.append((s, t0, w))
                t0 += w
        gi = depth + li
        wdn_i = 0
        sqv = hid[:, 0:NDT, :]
        for bi, (s, t0, w) in enumerate(blocks):
            i = bi % 2
            W2 = w + 2
            base = s * L
            lo = 1 if t0 == 0 else 0
            hi = W2 - 1 if t0 + w == L else W2
            hkeys = [('hT', i, dt) for dt in range(NDT)]
            S.add('sp', lambda e, i=i, lo=lo, hi=hi, base=base, t0=t0: e.dma_start(out=hT[i][:, :, lo:hi], in_=hsrc[:, :, base + t0 - 1 + lo: base + t0 - 1 + hi]),
                  w=hkeys, dma=True)
            if lo == 1:
                S.add('dve', lambda e, i=i: e.memset(hT[i][:, :, 0:1], 0.0), w=hkeys)
            if hi == W2 - 1:
                S.add('dve', lambda e, i=i, W2=W2: e.memset(hT[i][:, :, W2 - 1:W2], 0.0), w=hkeys)
            self.rmsnorm_fm(hT[i], hkeys, W2, gi, sqv, [('hid', dt) for dt in range(NDT)], self.ps[0], ('ps', 0), rtmp, rstd,
                            (lambda dt, W2=W2: hn[:, dt, 0:W2]), [('hn', dt) for dt in range(NDT)], eps_t)
            for fi in range(NFI):
                j = fi % 2
                pg, pv = self.ps[1 + 2 * j], self.ps[2 + 2 * j]
                for half, pbank, pkey in ((0, pg, ('ps', 1 + 2 * j)), (1, pv, ('ps', 2 + 2 * j))):
                    c0 = (half * NFI + fi) * 128
                    for dt in range(NDT):
                        S.add('pe', lambda e, dt=dt, c0=c0, pbank=pbank, W2=W2: e.matmul(pbank[:, 0:W2], lhsT=wup[:, dt, c0:c0 + 128], rhs=hn[:, dt, 0:W2], start=(dt == 0), stop=(dt == NDT - 1)),
                              r=[('wup', dt), ('hn', dt)], w=[pkey])
                halves = ((0, pg, ('ps', 1 + 2 * j), cg[j], ('cg', j)), (1, pv, ('ps', 2 + 2 * j), cvt[j], ('cvt', j)))
                for half, pbank, pkey, dstt, dkey in halves:
                    fc = half * NFI + fi
                    S.add('act', lambda e, pbank=pbank, dstt=dstt, fc=fc, w=w: e.activation(out=dstt[:, 0:w], in_=pbank[:, 1:w + 1], func=AF.Identity, scale=cw[:, 1, fc:fc + 1], bias=cb[:, fc:fc + 1]),
                          r=[pkey, ('cw', 1), 'cb'], w=[dkey])
                for tap, c_lo in ((0, 0), (2, 2)):
                    for half, pbank, pkey, dstt, dkey in halves:
                        fc = half * NFI + fi
                        S.add('dve', lambda e, pbank=pbank, dstt=dstt, fc=fc, w=w, tap=tap, c_lo=c_lo: e.scalar_tensor_tensor(out=dstt[:, 0:w], in0=pbank[:, c_lo:c_lo + w], scalar=cw[:, tap, fc:fc + 1], in1=dstt[:, 0:w], op0=ALU.mult, op1=ALU.add),
                              r=[pkey, ('cw', tap), dkey], w=[dkey])
                S.add('act', lambda e, j=j, w=w: e.activation(out=cg[j][:, 0:w], in_=cg[j][:, 0:w], func=AF.Silu), r=[('cg', j)], w=[('cg', j)])
                S.add('pool', lambda e, j=j, fi=fi, w=w: e.tensor_tensor(out=hid[:, fi, 0:w], in0=cg[j][:, 0:w], in1=cvt[j][:, 0:w], op=ALU.mult),
                      r=[('cg', j), ('cvt', j)], w=[('hid', fi)])
            for dt in range(NDT):
                k = wdn_i % 3
                wdn_i += 1
                S.add('sp', lambda e, k=k, dt=dt: e.dma_start(out=wdn[k][:], in_=wdsrc[dt]), r=[('wdn_s', li, fi) for fi in range(NFI)], w=[('wdn', k)], dma=True)
                pbank = self.ps[5 + dt % 2]
                pkey = ('ps', 5 + dt % 2)
                for fi in range(NFI):
                    S.add('pe', lambda e, k=k, fi=fi, pbank=pbank, w=w: e.matmul(pbank[:, 0:w], lhsT=wdn[k][:, fi, :], rhs=hid[:, fi, 0:w], start=(fi == 0), stop=(fi == NFI - 1)),
                          r=[('wdn', k), ('hid', fi)], w=[pkey])
                S.add('dve', lambda e, i=i, dt=dt, pbank=pbank, w=w: e.tensor_tensor(out=hT[i][:, dt, 1:w + 1], in0=pbank[:, 0:w], in1=hT[i][:, dt, 1:w + 1], op=ALU.add),
                      r=[pkey, ('hT', i, dt)], w=[('hT', i, dt)])
            S.add('sp', lambda e, i=i, base=base, t0=t0, w=w: e.dma_start(out=hdst[:, :, base + t0: base + t0 + w], in_=hT[i][:, :, 1:w + 1]),
                  r=hkeys, w=[('h', id(self.hnxt), bi)], dma=True)
        S.barrier()
        self.hcur, self.hnxt = self.hnxt, self.hcur

    def convert_attn_weights(self, j):
        wq = self.attn_w_qkv.ap()[j].rearrange("(dt p) n -> dt p n", p=128)
        dq = self.wqkv_s[j].ap().rearrange("(dt p) n -> dt p n", p=128)
        for dt in range(NDT):
            for c0 in range(0, 3 * D, 1024):
                self._conv_chunk((lambda dt=dt, c0=c0: wq[dt, :, c0:c0 + 1024]), (lambda b, dt=dt, c0=c0: dq[dt, :, c0:c0 + 1024]), 1024, ('wqkv_s', j, dt, c0))
        wo = self.attn_w_o.ap()[j].rearrange("(dt p) n -> dt p n", p=128)
        do = self.wo_s[j].ap().rearrange("(dt p) n -> dt p n", p=128)
        for dt in range(NDT):
            self._conv_chunk((lambda dt=dt: wo[dt]), (lambda b, dt=dt: do[dt]), 1024, ('wo_s', j, dt))

    def phase_attn(self, li):
        self.phase_attn_a(li)
        self.phase_attn_b(li)
        self.phase_attn_c(li)
        self.hcur, self.hnxt = self.hnxt, self.hcur

    def phase_attn_a(self, li):
        nc, S = self.nc, self.S
        j = li // 2
        L = self.L
        NTOK = self.NTOK
        nblk = NTOK // 512
        lambda_init = 0.8 - 0.6 * math.exp(-0.3 * li)
        A = self.top.sub()
        wqkv = A.tile("wqkv", [128, NDT, 3 * D], BF16)
        hT = [A.tile("hT", [128, NDT, 512], F32) for _ in range(2)]
        hn = A.tile("hn", [128, NDT, 512], BF16)
        sq = A.tile("sq", [128, NDT, 512], BF16)
        rtmp = A.tile("rtmp", [128, 512], F32)
        rstd = A.tile("rstd", [128, 512], F32)
        qstg = [A.tile("qstg", [128, 16, 512], BF16) for _ in range(2)]
        vstg = [A.tile("vstg", [128, 4, D], BF16) for _ in range(2)]
        eps_t = A.tile("eps", [128, 1], F32)
        S.add('dve', lambda e: e.memset(eps_t[:], NORM_EPS), w=['eps'])
        wsrc = self.wqkv_s[j].ap().rearrange("(dt p) n -> dt p n", p=128)
        for dt in range(NDT):
            S.add('sp', lambda e, dt=dt: e.dma_start(out=wqkv[:, dt, :], in_=wsrc[dt]),
                  r=[('wqkv_s', j, dt, c0) for c0 in range(0, 3 * D, 1024)], w=[('wqkv', dt)], dma=True)
        hsrc = self.hcur.ap().rearrange("(dt p) t -> p dt t", p=128)
        qdst = self.QT.ap().rearrange("(nt p) t -> p nt t", p=128)
        kdst = self.KT.ap().rearrange("(nt p) t -> p nt t", p=128)
        vdst = self.Vs.ap().rearrange("(b tt p) n -> b p tt n", p=128, tt=4)
        ev = 0
        for b in range(nblk):
            i = b % 2
            hkeys = [('hT', i, dt) for dt in range(NDT)]
            S.add('sp', lambda e, i=i, b=b: e.dma_start(out=hT[i][:], in_=hsrc[:, :, b * 512:(b + 1) * 512]), w=hkeys, dma=True)
            self.rmsnorm_fm(hT[i], hkeys, 512, li, sq, [('sq', dt) for dt in range(NDT)], self.ps[0], ('ps', 0), rtmp, rstd,
                            (lambda dt: hn[:, dt, :]), [('hn', dt) for dt in range(NDT)], eps_t)
            for nt in range(16):
                bk = 1 + nt % 4
                for dt in range(NDT):
                    S.add('pe', lambda e, nt=nt, dt=dt, bk=bk: e.matmul(self.ps[bk], lhsT=wqkv[:, dt, nt * 128:(nt + 1) * 128], rhs=hn[:, dt, :], start=(dt == 0), stop=(dt == NDT - 1)),
                          r=[('wqkv', dt), ('hn', dt)], w=[('ps', bk)])
                ev += 1
                if ev % 2 == 0:
                    S.add('act', lambda e, i=i, nt=nt, bk=bk: e.copy(out=qstg[i][:, nt, :], in_=self.ps[bk]), r=[('ps', bk)], w=[('qstg', i, nt)])
                else:
                    S.add('dve', lambda e, i=i, nt=nt, bk=bk: e.tensor_copy(out=qstg[i][:, nt, :], in_=self.ps[bk]), r=[('ps', bk)], w=[('qstg', i, nt)])
            S.add('sp', lambda e, i=i, b=b: e.dma_start(out=qdst[:, :, b * 512:(b + 1) * 512], in_=qstg[i][:, 0:8, :]), r=[('qstg', i, nt) for nt in range(8)], w=[('QT', b)], dma=True)
            S.add('sp', lambda e, i=i, b=b: e.dma_start(out=kdst[:, :, b * 512:(b + 1) * 512], in_=qstg[i][:, 8:16, :]), r=[('qstg', i, nt) for nt in range(8, 16)], w=[('KT', b)], dma=True)
            for tt in range(4):
                for half in range(2):
                    bk = 5 + (tt * 2 + half) % 3
                    for dt in range(NDT):
                        S.add('pe', lambda e, tt=tt, half=half, dt=dt, bk=bk: e.matmul(self.ps[bk], lhsT=hn[:, dt, tt * 128:(tt + 1) * 128], rhs=wqkv[:, dt, 2 * D + half * 512: 2 * D + (half + 1) * 512], start=(dt == 0), stop=(dt == NDT - 1)),
                              r=[('wqkv', dt), ('hn', dt)], w=[('ps', bk)])
                    ev += 1
                    if ev % 2 == 0:
                        S.add('act', lambda e, i=i, tt=tt, half=half, bk=bk: e.copy(out=vstg[i][:, tt, half * 512:(half + 1) * 512], in_=self.ps[bk]), r=[('ps', bk)], w=[('vstg', i, tt, half)])
                    else:
                        S.add('dve', lambda e, i=i, tt=tt, half=half, bk=bk: e.tensor_copy(out=vstg[i][:, tt, half * 512:(half + 1) * 512], in_=self.ps[bk]), r=[('ps', bk)], w=[('vstg', i, tt, half)])
            S.add('sp', lambda e, i=i, b=b: e.dma_start(out=vdst[b], in_=vstg[i][:]), r=[('vstg', i, tt, h) for tt in range(4) for h in range(2)], w=[('Vs', b)], dma=True)
        S.barrier()

    def phase_attn_b(self, li):
        nc, S = self.nc, self.S
        j = li // 2
        L = self.L
        NTOK = self.NTOK
        nblk = NTOK // 512
        lambda_init = 0.8 - 0.6 * math.exp(-0.3 * li)
        A = self.top.sub()
        KTh = [A.tile("KTh", [128, L], BF16) for _ in range(2)]
        QTh = [A.tile("QTh", [128, L], BF16) for _ in range(2)]
        Vh = [A.tile("Vh", [128, L // 128, 128], BF16) for _ in range(2)]
        tab = [A.tile("tab", [128, 1152], F32) for _ in range(2)]
        far = A.tile("far", [128, 16], F32)
        PT = [A.tile("PT", [128, 2, 512], BF16) for _ in range(3)]
        tmpb = [A.tile("tmpb", [128, 2, 512], F32) for _ in range(2)]
        r1 = A.tile("r1", [128, 512], F32)
        r2 = A.tile("r2", [128, 512], F32)
        oa = A.tile("oa", [128, 512], F32)
        ob = A.tile("ob", [128, 512], F32)
        osq = A.tile("osq", [128, 512], BF16)
        lnt = A.tile("lnt", [128, 512], F32)
        rs2 = A.tile("rs2", [128, 512], F32)
        aout = [A.tile("aout", [128, 512], BF16) for _ in range(2)]
        lqk = A.tile("lqk", [64, 4], F32)
        lprod = A.tile("lprod", [64, 2], F32)
        ones_f = A.tile("ones_f", [128, 128], F32)
        acc = [[A.tile("acc", [128, 2, 512], F32) for _ in range(2)] for _ in range(2)]
        lam_t = A.tile("lam_t", [128, 4], F32)
        gsub = A.tile("gsub", [128, 1], F32)
        eps2 = A.tile("eps2", [128, 1], F32)
        for c, src in enumerate((self.attn_lq1, self.attn_lk1, self.attn_lq2, self.attn_lk2)):
            S.add('sp', lambda e, c=c, src=src: e.dma_start(out=lqk[:, c:c + 1], in_=src.ap()[j].rearrange("(p o) -> p o", o=1)), w=[('lqk', c)], dma=True)
        S.add('sp', lambda e: e.dma_start(out=gsub[:], in_=self.attn_subln_g.ap()[j].rearrange("(p o) -> p o", o=1)), w=['gsub'], dma=True)
        S.add('sp', lambda e: e.dma_start(out=far[:], in_=self.relfar.ap()), w=['far'], dma=True)
        S.add('dve', lambda e: e.memset(ones_f[:], 1.0), w=['ones_f'])
        S.add('dve', lambda e: e.memset(eps2[:], SUBLN_EPS), w=['eps2'])
        S.add('dve', lambda e: e.tensor_tensor(out=lprod[:, 0:1], in0=lqk[:, 0:1], in1=lqk[:, 1:2], op=ALU.mult), r=[('lqk', 0), ('lqk', 1)], w=[('lprod', 0)])
        S.add('dve', lambda e: e.tensor_tensor(out=lprod[:, 1:2], in0=lqk[:, 2:3], in1=lqk[:, 3:4], op=ALU.mult), r=[('lqk', 2), ('lqk', 3)], w=[('lprod', 1)])
        S.add('pe', lambda e: e.matmul(self.ps[0][:, 0:2], lhsT=ones_f[0:64, :], rhs=lprod[:], start=True, stop=True), r=['ones_f', ('lprod', 0), ('lprod', 1)], w=[('ps', 0)])
        S.add('act', lambda e: e.activation(out=lam_t[:, 0:2], in_=self.ps[0][:, 0:2], func=AF.Exp), r=[('ps', 0)], w=['lam_e'])
        S.add('dve', lambda e: e.tensor_tensor(out=lam_t[:, 2:3], in0=lam_t[:, 1:2], in1=lam_t[:, 0:1], op=ALU.subtract), r=['lam_e'], w=['lam_d'])
        S.add('dve', lambda e: e.tensor_scalar(out=lam_t[:, 3:4], in0=lam_t[:, 2:3], scalar1=-lambda_init, scalar2=None, op0=ALU.add), r=['lam_d'], w=['neglam'])
        S.add('dve', lambda e: e.tensor_scalar(out=gsub[:], in0=gsub[:], scalar1=(1.0 - lambda_init), scalar2=None, op0=ALU.mult), r=['gsub'], w=['gsub'])
        NKT = L // 128
        NQB = L // 512
        atd = self.AT.ap().rearrange("(h p) t -> h p t", p=128)
        heads = [(s_, h) for s_ in range(self.NSEQ) for h in range(8)]
        its = []
        for hi, (s_, h) in enumerate(heads):
            for qb in range(NQB):
                for kt in range(NKT):
                    its.append((hi, s_, h, qb, kt))

        def emit_head_load(hi):
            s_, h = heads[hi]
            i = hi % 2
            base = s_ * L
            S.add('sp', lambda e: e.dma_start(out=KTh[i][:], in_=self.KT.ap()[h * 128:(h + 1) * 128, base:base + L]),
                  r=[('KT', b) for b in range(nblk)], w=[('KTh', i)], dma=True)
            S.add('sp', lambda e: e.dma_start(out=QTh[i][:], in_=self.QT.ap()[h * 128:(h + 1) * 128, base:base + L]),
                  r=[('QT', b) for b in range(nblk)], w=[('QTh', i)], dma=True)
            S.add('sp', lambda e: e.dma_start(out=Vh[i][:], in_=self.Vs.ap()[base:base + L, h * 128:(h + 1) * 128].rearrange("(kt p) e -> p kt e", p=128)),
                  r=[('Vs', b) for b in range(nblk)], w=[('Vh', i)], dma=True)
            S.add('sp', lambda e: e.dma_start(out=tab[i][:], in_=self.reltab.ap()[h]), w=[('tab', i)], dma=True)

        def emit_scores(idx):
            hi, s_, h, qb, kt = its[idx]
            i = hi % 2
            sb = 2 * (idx % 2)
            for t in range(2):
                S.add('pe', lambda e, t=t: e.matmul(self.ps[sb + t], lhsT=KTh[i][t * 64:(t + 1) * 64, kt * 128:(kt + 1) * 128], rhs=QTh[i][t * 64:(t + 1) * 64, qb * 512:(qb + 1) * 512], start=True, stop=True),
                      r=[('KTh', i), ('QTh', i)], w=[('ps', sb + t)])

        tbi = [0]

        def emit_exp(idx):
            hi, s_, h, qb, kt = its[idx]
            i = hi % 2
            sb = 2 * (idx % 2)
            p_i = idx % 3
            delta = kt * 128 - qb * 512
            if -128 <= delta <= 512:
                tb = tbi[0] % 2
                tbi[0] += 1
                x0 = 512 - delta
                S.add('dve', lambda e: e.scalar_tensor_tensor(out=tmpb[tb][:], in0=self.psall[:, sb:sb + 2, :], scalar=0.125,
                                                             in1=tab[i][:, x0:x0 + 512].unsqueeze(1).to_broadcast([128, 2, 512]), op0=ALU.mult, op1=ALU.add),
                      r=[('ps', sb), ('ps', sb + 1), ('tab', i)], w=[('tmpb', tb)])
                S.add('act', lambda e: e.activation(out=PT[p_i][:], in_=tmpb[tb][:], func=AF.Exp), r=[('tmpb', tb)], w=[('PT', p_i)])
            else:
                side = 0 if delta < 0 else 1
                S.add('act', lambda e: e.activation(out=PT[p_i][:], in_=self.psall[:, sb:sb + 2, :], func=AF.Exp, scale=0.125, bias=far[:, h * 2 + side:h * 2 + side + 1]),
                      r=[('ps', sb), ('ps', sb + 1), 'far'], w=[('PT', p_i)])

        def emit_pv(idx):
            hi, s_, h, qb, kt = its[idx]
            i = hi % 2
            p_i = idx % 3
            for t in range(2):
                S.add('pe', lambda e, t=t: e.matmul(self.ps[4 + t], lhsT=Vh[i][:, kt, :], rhs=PT[p_i][:, t, :], start=(kt == 0), stop=(kt == NKT - 1)),
                      r=[('Vh', i), ('PT', p_i)], w=[('ps', 4 + t)])
            qpar = (hi * NQB + qb) % 2
            en = kt % 2
            eng = 'dve' if en == 0 else 'pool'
            a_t = acc[qpar][en]
            if kt < 2:
                S.add(eng, lambda e: e.tensor_copy(out=a_t[:], in_=PT[p_i][:]), r=[('PT', p_i)], w=[('acc', qpar, en)])
            else:
                S.add(eng, lambda e: e.tensor_tensor(out=a_t[:], in0=a_t[:], in1=PT[p_i][:], op=ALU.add), r=[('PT', p_i), ('acc', qpar, en)], w=[('acc', qpar, en)])

        aoi = [0]

        def emit_epilogue(idx):
            hi, s_, h, qb, kt = its[idx]
            base = s_ * L
            ao = aoi[0] % 2
            aoi[0] += 1
            qpar = (hi * NQB + qb) % 2
            for t in range(2):
                for en in range(2):
                    S.add('pe', lambda e, t=t, en=en: e.matmul(self.ps[6 + t], lhsT=ones_f[:], rhs=acc[qpar][en][:, t, :], start=(en == 0), stop=(en == 1)),
                          r=['ones_f', ('acc', qpar, en)], w=[('ps', 6 + t)])
            S.add('dve', lambda e: e.reciprocal(out=r1[:], in_=self.ps[6]), r=[('ps', 6)], w=['r1'])
            S.add('dve', lambda e: e.tensor_tensor(out=oa[:], in0=self.ps[4], in1=r1[:], op=ALU.mult), r=[('ps', 4), 'r1'], w=['oa'])
            S.add('dve', lambda e: e.reciprocal(out=r2[:], in_=self.ps[7]), r=[('ps', 7)], w=['r2'])
            S.add('dve', lambda e: e.scalar_tensor_tensor(out=ob[:], in0=self.ps[5], scalar=lam_t[:, 3:4], in1=r2[:], op0=ALU.mult, op1=ALU.mult), r=[('ps', 5), 'r2', 'neglam'], w=['ob'])
            S.add('dve', lambda e: e.tensor_tensor(out=aout[ao][:], in0=oa[:], in1=ob[:], op=ALU.add), r=['oa', 'ob'], w=[('aout', ao)])
            S.add('sp', lambda e: e.dma_start(out=atd[h, :, base + qb * 512: base + (qb + 1) * 512], in_=aout[ao][:]), r=[('aout', ao)], w=[('AT', s_, h, qb)], dma=True)

        emit_head_load(0)
        loaded = 0
        for idx in range(len(its)):
            hi = its[idx][0]
            if idx == 0:
                emit_scores(0)
            if (idx == 0 or its[idx - 1][0] != hi) and hi + 1 < len(heads) and loaded < hi + 1:
                emit_head_load(hi + 1)
                loaded = hi + 1
            if idx + 1 < len(its):
                emit_scores(idx + 1)
            emit_exp(idx)
            emit_pv(idx)
            if its[idx][4] == NKT - 1:
                emit_epilogue(idx)
        S.barrier()

    def phase_attn_c(self, li):
        nc, S = self.nc, self.S
        j = li // 2
        L = self.L
        NTOK = self.NTOK
        nblk = NTOK // 512
        lambda_init = 0.8 - 0.6 * math.exp(-0.3 * li)
        A = self.top.sub()
        wo = A.tile("wo", [128, NDT, D], BF16)
        hT = [A.tile("hT", [128, NDT, 512], F32) for _ in range(2)]
        at = [A.tile("at", [128, NDT, 512], BF16) for _ in range(2)]
        atn = A.tile("atn", [128, NDT, 512], BF16)
        sqt = A.tile("sqt", [128, NDT, 512], BF16)
        lnt = [A.tile("lnt", [128, 512], F32) for _ in range(2)]
        rs2 = [A.tile("rs2", [128, 512], F32) for _ in range(2)]
        gsub = A.tile("gsub", [128, 1], F32)
        eps2 = A.tile("eps2", [128, 1], F32)
        S.add('sp', lambda e: e.dma_start(out=gsub[:], in_=self.attn_subln_g.ap()[j].rearrange("(p o) -> p o", o=1)), w=['gsub'], dma=True)
        S.add('dve', lambda e: e.memset(eps2[:], SUBLN_EPS), w=['eps2'])
        S.add('dve', lambda e: e.tensor_scalar(out=gsub[:], in0=gsub[:], scalar1=(1.0 - lambda_init), scalar2=None, op0=ALU.mult), r=['gsub'], w=['gsub'])
        wsrc = self.wo_s[j].ap().rearrange("(dt p) n -> dt p n", p=128)
        for dt in range(NDT):
            S.add('sp', lambda e, dt=dt: e.dma_start(out=wo[:, dt, :], in_=wsrc[dt]), r=[('wo_s', j, dt)], w=[('wo', dt)], dma=True)
        hsrc = self.hcur.ap().rearrange("(dt p) t -> p dt t", p=128)
        hdst = self.hnxt.ap().rearrange("(dt p) t -> p dt t", p=128)
        asrc = self.AT.ap().rearrange("(h p) t -> p h t", p=128)
        for b in range(nblk):
            i = b % 2
            hkeys = [('hT', i, dt) for dt in range(NDT)]
            S.add('sp', lambda e, i=i, b=b: e.dma_start(out=hT[i][:], in_=hsrc[:, :, b * 512:(b + 1) * 512]), w=hkeys, dma=True)
            S.add('sp', lambda e, i=i, b=b: e.dma_start(out=at[i][:], in_=asrc[:, :, b * 512:(b + 1) * 512]), w=[('at', i)], dma=True)
            S.add('act', lambda e, i=i: e.activation(out=sqt[:], in_=at[i][:], func=AF.Square), r=[('at', i)], w=['sqt'])
            for hh in range(NDT):
                bk = 4 + hh % 4
                k2 = hh % 2
                S.add('pe', lambda e, hh=hh, bk=bk: e.matmul(self.ps[bk], lhsT=self.ones_bf[:], rhs=sqt[:, hh, :], start=True, stop=True), r=['sqt'], w=[('ps', bk)])
                S.add('act', lambda e, bk=bk, k2=k2: e.activation(out=lnt[k2][:], in_=self.ps[bk], func=AF.Ln, scale=1.0 / 128, bias=eps2[:, 0:1]), r=[('ps', bk), 'eps2'], w=[('lnt', k2)])
                S.add('act', lambda e, k2=k2: e.activation(out=rs2[k2][:], in_=lnt[k2][:], func=AF.Exp, scale=-0.5), r=[('lnt', k2)], w=[('rs2', k2)])
                S.add('dve', lambda e, i=i, hh=hh, k2=k2: e.scalar_tensor_tensor(out=atn[:, hh, :], in0=at[i][:, hh, :], scalar=gsub[:, 0:1], in1=rs2[k2][:], op0=ALU.mult, op1=ALU.mult),
                      r=[('at', i), 'gsub', ('rs2', k2)], w=[('atn', hh)])
            for nt in range(NDT):
                bk = nt % 4
                for hh in range(NDT):
                    S.add('pe', lambda e, i=i, nt=nt, hh=hh, bk=bk: e.matmul(self.ps[bk], lhsT=wo[:, hh, nt * 128:(nt + 1) * 128], rhs=atn[:, hh, :], start=(hh == 0), stop=(hh == NDT - 1)),
                          r=[('wo', hh), ('atn', hh)], w=[('ps', bk)])
                S.add('dve', lambda e, i=i, nt=nt, bk=bk: e.tensor_tensor(out=hT[i][:, nt, :], in0=self.ps[bk], in1=hT[i][:, nt, :], op=ALU.add), r=[('ps', bk), ('hT', i, nt)], w=[('hT', i, nt)])
            S.add('sp', lambda e, i=i, b=b: e.dma_start(out=hdst[:, :, b * 512:(b + 1) * 512], in_=hT[i][:]), r=hkeys, w=[('h2', b)], dma=True)
        S.barrier()

    def convert_s5_weights(self, j):
        wg = self.s5_glu_w.ap()[j].rearrange("(dt p) n -> dt p n", p=128)
        dg = self.wglu_s[j].ap().rearrange("(dt p) n -> dt p n", p=128)
        for dt in range(NDT):
            self._conv_chunk((lambda dt=dt: wg[dt]), (lambda b, dt=dt: dg[dt]), 1024, ('wglu_s', j, dt))

    def phase_s5(self, li):
        self.s5_prep1(li)
        self.s5_prep2(li)
        self.s5_main(li)
        self.hcur, self.hnxt = self.hnxt, self.hcur

    def s5_prep1(self, li):
        nc, S = self.nc, self.S
        j = li // 2
        P = self.top.sub()
        self.s5P = P
        T = {}
        self.s5T = T
        ap_re = P.tile("ap_re", [64, 9, 128], F32); ap_im = P.tile("ap_im", [64, 9, 128], F32)
        ai_re = P.tile("ai_re", [64, 9, 128], F32); ai_im = P.tile("ai_im", [64, 9, 128], F32)
        Bre = P.tile("Bre", [64, 128, 16], F32); Bim = P.tile("Bim", [64, 128, 16], F32)
        CTre = P.tile("CTre", [64, 128, 16], F32); CTim = P.tile("CTim", [64, 128, 16], F32)
        maskF = P.tile("maskF", [128, 128], F32); maskB = P.tile("maskB", [128, 128], F32)
        dcol = P.tile("dcol", [128, 64], F32)
        T.update(ap_re=ap_re, ap_im=ap_im, ai_re=ai_re, ai_im=ai_im, Bre=Bre, Bim=Bim, CTre=CTre, CTim=CTim, maskF=maskF, maskB=maskB, dcol=dcol)
        Q = P.sub()
        lre = Q.tile("lre", [64, 128], F32); lim = Q.tile("lim", [64, 128], F32); lst = Q.tile("lst", [64, 128], F32)
        t = [Q.tile("t%d" % k, [64, 128], F32) for k in range(8)]
        ti = Q.tile("ti", [64, 128], I32)
        fr = Q.tile("fr", [64, 128], F32); fi = Q.tile("fi", [64, 128], F32)
        tA = Q.tile("tA", [64, 128, 16], F32); tB = Q.tile("tB", [64, 128, 16], F32)
        Cld = [Q.tile("Cld", [128, 16, 64], F32) for _ in range(2)]
        kst = Q.tile("kst", [64, 128, 27], F32)
        kr = Q.tile("kr", [64, 128], F32); ki = Q.tile("ki", [64, 128], F32)
        zero_b = Q.tile("zerob", [64, 1], F32)
        dv = lambda fn, r, w: S.add('dve', fn, r=r, w=w)
        S.add('sp', lambda e: e.dma_start(out=lre[:], in_=self.s5_lre.ap()[j].rearrange("d g p -> p (d g)"), allow_slow_non_contiguous=True), w=['lre'], dma=True)
        S.add('sp', lambda e: e.dma_start(out=lim[:], in_=self.s5_lim.ap()[j].rearrange("d g p -> p (d g)"), allow_slow_non_contiguous=True), w=['lim'], dma=True)
        S.add('sp', lambda e: e.dma_start(out=lst[:], in_=self.s5_lst.ap()[j].rearrange("d g -> (d g)").partition_broadcast(64)), w=['lst'], dma=True)
        S.add('sp', lambda e: e.dma_start(out=Bre[:], in_=self.s5_bre.ap()[j].rearrange("d g p c -> p (d g) c")), w=['Bre'], dma=True)
        S.add('sp', lambda e: e.dma_start(out=Bim[:], in_=self.s5_bim.ap()[j].rearrange("d g p c -> p (d g) c")), w=['Bim'], dma=True)
        for d_ in range(2):
            S.add('sp', lambda e, d_=d_: e.dma_start(out=Cld[0][:, d_ * 8:(d_ + 1) * 8, :], in_=self.s5_cre.ap()[j, d_].rearrange("(go gs) c p -> (gs c) go p", gs=8)), w=[('Cld', 0, d_)], dma=True)
            S.add('sp', lambda e, d_=d_: e.dma_start(out=Cld[1][:, d_ * 8:(d_ + 1) * 8, :], in_=self.s5_cim.ap()[j, d_].rearrange("(go gs) c p -> (gs c) go p", gs=8)), w=[('Cld', 1, d_)], dma=True)
        S.add('sp', lambda e: e.dma_start(out=maskF[:], in_=self.maskF_in.ap()), w=['maskF'], dma=True)
        S.add('sp', lambda e: e.dma_start(out=maskB[:], in_=self.maskB_in.ap()), w=['maskB'], dma=True)
        for tau in range(8):
            S.add('sp', lambda e, tau=tau: e.dma_start(out=dcol[tau * 16:(tau + 1) * 16, :], in_=self.s5_d.ap()[j].rearrange("(g c) -> c g", c=16), allow_slow_non_contiguous=True), w=[('dcol', tau)], dma=True)
        dv(lambda e: e.memset(zero_b[:], 0.0), [], ['zerob'])
        dv(lambda e: e.tensor_scalar(out=lre[:], in0=lre[:], scalar1=-1e-4, scalar2=None, op0=ALU.min), ['lre'], ['lre'])
        S.add('act', lambda e: e.activation(out=lst[:], in_=lst[:], func=AF.Exp), r=['lst'], w=['lst'])
        dv(lambda e: e.tensor_tensor(out=t[0][:], in0=lst[:], in1=lre[:], op=ALU.mult), ['lst', 'lre'], ['t0'])
        dv(lambda e: e.tensor_tensor(out=t[1][:], in0=lst[:], in1=lim[:], op=ALU.mult), ['lst', 'lim'], ['t1'])
        S.add('act', lambda e: e.activation(out=t[2][:], in_=t[0][:], func=AF.Exp), r=['t0'], w=['t2'])
        S.add('act', lambda e: e.activation(out=t[7][:], in_=t[0][:], func=AF.Exp, scale=-1.0), r=['t0'], w=['t7'])
        TWO_PI = 2.0 * math.pi
        for which, off, dst in ((0, 0.0, 3), (1, 0.5 * math.pi, 4)):
            dv(lambda e, off=off: e.tensor_scalar(out=t[5][:], in0=t[1][:], scalar1=off, scalar2=1.0 / TWO_PI, op0=ALU.add, op1=ALU.mult), ['t1'], ['t5'])
            dv(lambda e: e.tensor_copy(out=ti[:], in_=t[5][:]), ['t5'], ['ti'])
            dv(lambda e: e.tensor_copy(out=t[5][:], in_=ti[:]), ['ti'], ['t5'])
            dv(lambda e, off=off: e.tensor_scalar(out=t[6][:], in0=t[1][:], scalar1=off, scalar2=None, op0=ALU.add), ['t1'], ['t6'])
            dv(lambda e: e.scalar_tensor_tensor(out=t[6][:], in0=t[5][:], scalar=-TWO_PI, in1=t[6][:], op0=ALU.mult, op1=ALU.add), ['t5', 't6'], ['t6'])
            S.add('act', lambda e, dst=dst: e.activation(out=t[dst][:], in_=t[6][:], func=AF.Sin, bias=zero_b[:, 0:1]), r=['t6', 'zerob'], w=['t%d' % dst])
        dv(lambda e: e.memset(ap_re[:, 0, :], 1.0), [], [('ap', 0)])
        dv(lambda e: e.memset(ap_im[:, 0, :], 0.0), [], [('api', 0)])
        dv(lambda e: e.memset(ai_re[:, 0, :], 1.0), [], [('ai', 0)])
        dv(lambda e: e.memset(ai_im[:, 0, :], 0.0), [], [('aii', 0)])
        dv(lambda e: e.tensor_tensor(out=ap_re[:, 1, :], in0=t[2][:], in1=t[4][:], op=ALU.mult), ['t2', 't4'], [('ap', 1)])
        dv(lambda e: e.tensor_tensor(out=ap_im[:, 1, :], in0=t[2][:], in1=t[3][:], op=ALU.mult), ['t2', 't3'], [('api', 1)])
        dv(lambda e: e.tensor_tensor(out=ai_re[:, 1, :], in0=t[7][:], in1=t[4][:], op=ALU.mult), ['t7', 't4'], [('ai', 1)])
        dv(lambda e: e.scalar_tensor_tensor(out=ai_im[:, 1, :], in0=t[7][:], scalar=-1.0, in1=t[3][:], op0=ALU.mult, op1=ALU.mult), ['t7', 't3'], [('aii', 1)])
        for (pr, pi, kr_, ki_) in ((ap_re, ap_im, 'ap', 'api'), (ai_re, ai_im, 'ai', 'aii')):
            for k in range(2, 9):
                dv(lambda e, pr=pr, pi=pi, k=k: e.tensor_tensor(out=t[5][:], in0=pr[:, k - 1, :], in1=pr[:, 1, :], op=ALU.mult), [(kr_, k - 1), (kr_, 1)], ['t5'])
                dv(lambda e, pr=pr, pi=pi, k=k: e.tensor_tensor(out=t[6][:], in0=pi[:, k - 1, :], in1=pi[:, 1, :], op=ALU.mult), [(ki_, k - 1), (ki_, 1)], ['t6'])
                dv(lambda e, pr=pr, k=k: e.tensor_tensor(out=pr[:, k, :], in0=t[5][:], in1=t[6][:], op=ALU.subtract), ['t5', 't6'], [(kr_, k)])
                dv(lambda e, pr=pr, pi=pi, k=k: e.tensor_tensor(out=t[5][:], in0=pr[:, k - 1, :], in1=pi[:, 1, :], op=ALU.mult), [(kr_, k - 1), (ki_, 1)], ['t5'])
                dv(lambda e, pr=pr, pi=pi, k=k: e.tensor_tensor(out=t[6][:], in0=pi[:, k - 1, :], in1=pr[:, 1, :], op=ALU.mult), [(ki_, k - 1), (kr_, 1)], ['t6'])
                dv(lambda e, pi=pi, k=k: e.tensor_tensor(out=pi[:, k, :], in0=t[5][:], in1=t[6][:], op=ALU.add), ['t5', 't6'], [(ki_, k)])
        dv(lambda e: e.tensor_scalar(out=t[0][:], in0=ap_re[:, 1, :], scalar1=-1.0, scalar2=None, op0=ALU.add), [('ap', 1)], ['t0'])
        dv(lambda e: e.tensor_tensor(out=t[5][:], in0=lre[:], in1=lre[:], op=ALU.mult), ['lre'], ['t5'])
        dv(lambda e: e.tensor_tensor(out=t[6][:], in0=lim[:], in1=lim[:], op=ALU.mult), ['lim'], ['t6'])
        dv(lambda e: e.tensor_tensor(out=t[5][:], in0=t[5][:], in1=t[6][:], op=ALU.add), ['t5', 't6'], ['t5'])
        dv(lambda e: e.reciprocal(out=t[7][:], in_=t[5][:]), ['t5'], ['t7'])
        dv(lambda e: e.tensor_tensor(out=t[5][:], in0=t[0][:], in1=lre[:], op=ALU.mult), ['t0', 'lre'], ['t5'])
        dv(lambda e: e.tensor_tensor(out=t[6][:], in0=ap_im[:, 1, :], in1=lim[:], op=ALU.mult), [('api', 1), 'lim'], ['t6'])
        dv(lambda e: e.tensor_tensor(out=t[5][:], in0=t[5][:], in1=t[6][:], op=ALU.add), ['t5', 't6'], ['t5'])
        dv(lambda e: e.tensor_tensor(out=fr[:], in0=t[5][:], in1=t[7][:], op=ALU.mult), ['t5', 't7'], ['fr'])
        dv(lambda e: e.tensor_tensor(out=t[5][:], in0=ap_im[:, 1, :], in1=lre[:], op=ALU.mult), [('api', 1), 'lre'], ['t5'])
        dv(lambda e: e.tensor_tensor(out=t[6][:], in0=t[0][:], in1=lim[:], op=ALU.mult), ['t0', 'lim'], ['t6'])
        dv(lambda e: e.tensor_tensor(out=t[5][:], in0=t[5][:], in1=t[6][:], op=ALU.subtract), ['t5', 't6'], ['t5'])
        dv(lambda e: e.tensor_tensor(out=fi[:], in0=t[5][:], in1=t[7][:], op=ALU.mult), ['t5', 't7'], ['fi'])
        frb = lambda: fr[:].unsqueeze(2).to_broadcast([64, 128, 16])
        fib = lambda: fi[:].unsqueeze(2).to_broadcast([64, 128, 16])
        dv(lambda e: e.tensor_tensor(out=tA[:], in0=Bim[:], in1=fib(), op=ALU.mult), ['Bim', 'fi'], ['tA'])
        dv(lambda e: e.tensor_tensor(out=tB[:], in0=Bre[:], in1=fib(), op=ALU.mult), ['Bre', 'fi'], ['tB'])
        dv(lambda e: e.tensor_tensor(out=Bre[:], in0=Bre[:], in1=frb(), op=ALU.mult), ['Bre', 'fr', 'tB'], ['Bre'])
        dv(lambda e: e.tensor_tensor(out=Bre[:], in0=Bre[:], in1=tA[:], op=ALU.subtract), ['Bre', 'tA'], ['Bre'])
        dv(lambda e: e.tensor_tensor(out=Bim[:], in0=Bim[:], in1=frb(), op=ALU.mult), ['Bim', 'fr', 'tA'], ['Bim'])
        dv(lambda e: e.tensor_tensor(out=Bim[:], in0=Bim[:], in1=tB[:], op=ALU.add), ['Bim', 'tB'], ['Bim'])
        for ri, (cl, ct, nm) in enumerate(((Cld[0], CTre, 'CTre'), (Cld[1], CTim, 'CTim'))):
            for q in range(4):
                bk = (ri * 4 + q) % 4
                for r_ in range(4):
                    idx = q * 4 + r_
                    S.add('pe', lambda e, cl=cl, idx=idx, bk=bk, r_=r_: e.transpose(self.ps[bk][0:64, r_ * 128:(r_ + 1) * 128], cl[:, idx, :], self.ident[:]),
                          r=[('Cld', ri, idx // 8), 'ident'], w=[('ps', bk, r_)])
                S.add('act', lambda e, ct=ct, q=q, bk=bk: e.copy(out=ct[:, q * 32:(q + 1) * 32, :].rearrange("p g c -> p (g c)"), in_=self.ps[bk][0:64, :]),
                      r=[('ps', bk, r_) for r_ in range(4)], w=[(nm, q)])
        dv(lambda e: e.tensor_copy(out=kr[:], in_=ap_re[:, 8, :]), [('ap', 8)], ['kr'])
        dv(lambda e: e.tensor_copy(out=ki[:], in_=ap_im[:, 8, :]), [('api', 8)], ['ki'])
        for k in range(9):
            dv(lambda e, k=k: e.tensor_copy(out=kst[:, :, 3 * k], in_=kr[:]), ['kr'], [('kst', k, 0)])
            dv(lambda e, k=k: e.tensor_copy(out=kst[:, :, 3 * k + 1], in_=ki[:]), ['ki'], [('kst', k, 1)])
            dv(lambda e, k=k: e.tensor_scalar(out=kst[:, :, 3 * k + 2], in0=ki[:], scalar1=-1.0, scalar2=None, op0=ALU.mult), ['ki'], [('kst', k, 2)])
            if k < 8:
                dv(lambda e: e.tensor_tensor(out=t[5][:], in0=kr[:], in1=kr[:], op=ALU.mult), ['kr'], ['t5'])
                dv(lambda e: e.tensor_tensor(out=t[6][:], in0=ki[:], in1=ki[:], op=ALU.mult), ['ki'], ['t6'])
                dv(lambda e: e.tensor_tensor(out=t[0][:], in0=kr[:], in1=ki[:], op=ALU.mult), ['kr', 'ki'], ['t0'])
                dv(lambda e: e.tensor_tensor(out=kr[:], in0=t[5][:], in1=t[6][:], op=ALU.subtract), ['t5', 't6'], ['kr'])
                dv(lambda e: e.tensor_scalar(out=ki[:], in0=t[0][:], scalar1=2.0, scalar2=None, op0=ALU.mult), ['t0'], ['ki'])
        S.add('sp', lambda e: e.dma_start(out=self.KSd.ap().rearrange("p d u gi k -> p d (u gi) k"), in_=kst[:].rearrange("p (d g) k -> p d g k", d=2)),
              r=[('kst', k, c) for k in range(9) for c in range(3)], w=[('KSd',)], dma=True)
        S.barrier()

    def s5_prep2(self, li):
        nc, S = self.nc, self.S
        j = li // 2
        T = self.s5T
        ap_re, ap_im, ai_re, ai_im = T['ap_re'], T['ap_im'], T['ai_re'], T['ai_im']
        Bre, Bim, CTre, CTim = T['Bre'], T['Bim'], T['CTre'], T['CTim']
        maskF, maskB, dcol = T['maskF'], T['maskB'], T['dcol']
        Q = self.s5P.sub()
        GQ = 16
        tabs = {nm: Q.tile(nm, [64, GQ, 128], F32) for nm in ('Hre', 'Him', 'Fre', 'Fim', 'Wre', 'Wim')}
        Macc = Q.tile("Macc", [128, 64, 128], F32)
        tmpM = Q.tile("tmpM", [128, 128], F32)
        Wst = [Q.tile("Wst", [128, GQ, 2, 64], BF16) for _ in range(2)]
        Fst = [Q.tile("Fst", [64, GQ, 2, 128], BF16) for _ in range(2)]
        Mst = [Q.tile("Mst", [128, 32, 128], BF16) for _ in range(2)]
        t1 = Q.tile("t1", [64, GQ, 16], F32); t2 = Q.tile("t2", [64, GQ, 16], F32)
        dv = lambda fn, r, w: S.add('dve', fn, r=r, w=w)
        qi = 0
        for d_ in range(2):
            for gq in range(4):
                g0 = d_ * 64 + gq * GQ
                sl = slice(g0, g0 + GQ)
                bc = lambda a, k, sl=sl: a[:, k, sl].unsqueeze(2).to_broadcast([64, GQ, 16])

                def cmul(ore, oim, xre, xim, sre, sim, negim, keys_w, tag):
                    dv(lambda e: e.tensor_tensor(out=t1[:], in0=xre, in1=sre, op=ALU.mult), ['B', 'C', 'apw'], ['t1'])
                    dv(lambda e: e.tensor_tensor(out=t2[:], in0=xim, in1=sim, op=ALU.mult), ['B', 'C', 'apw'], ['t2'])
                    dv(lambda e: e.tensor_tensor(out=ore, in0=t1[:], in1=t2[:], op=ALU.subtract), ['t1', 't2'], [keys_w[0]])
                    dv(lambda e: e.tensor_tensor(out=t1[:], in0=xre, in1=sim, op=ALU.mult), ['B', 'C', 'apw'], ['t1'])
                    dv(lambda e: e.tensor_tensor(out=t2[:], in0=xim, in1=sre, op=ALU.mult), ['B', 'C', 'apw'], ['t2'])
                    if negim:
                        dv(lambda e: e.scalar_tensor_tensor(out=oim, in0=t1[:], scalar=-1.0, in1=t2[:], op0=ALU.mult, op1=ALU.subtract), ['t1', 't2'], [keys_w[1]])
                    else:
                        dv(lambda e: e.tensor_tensor(out=oim, in0=t1[:], in1=t2[:], op=ALU.add), ['t1', 't2'], [keys_w[1]])

                for sg in range(8):
                    kH = sg + 1 if d_ == 0 else 8 - sg
                    kW = 7 - sg if d_ == 0 else sg
                    kF = sg + 1 if d_ == 0 else 8 - sg
                    cs = slice(sg * 16, (sg + 1) * 16)
                    cmul(tabs['Hre'][:, :, cs], tabs['Him'][:, :, cs], Bre[:, sl, :], Bim[:, sl, :], bc(ai_re, kH), bc(ai_im, kH), False, ['Hre', 'Him'], 'H')
                    cmul(tabs['Wre'][:, :, cs], tabs['Wim'][:, :, cs], Bre[:, sl, :], Bim[:, sl, :], bc(ap_re, kW), bc(ap_im, kW), False, ['Wre', 'Wim'], 'W')
                    cmul(tabs['Fre'][:, :, cs], tabs['Fim'][:, :, cs], CTre[:, sl, :], CTim[:, sl, :], bc(ap_re, kF), bc(ap_im, kF), True, ['Fre', 'Fim'], 'F')
                wi = qi % 2
                qi += 1
                for gl in range(GQ):
                    g = gq * GQ + gl
                    bk = gl % 2
                    S.add('pe', lambda e, gl=gl, bk=bk: e.matmul(self.ps[bk][:, 0:128], lhsT=tabs['Hre'][:, gl, :], rhs=tabs['Fre'][:, gl, :], start=True, stop=False), r=['Hre', 'Fre'], w=[('ps', bk)])
                    S.add('pe', lambda e, gl=gl, bk=bk: e.matmul(self.ps[bk][:, 0:128], lhsT=tabs['Him'][:, gl, :], rhs=tabs['Fim'][:, gl, :], start=False, stop=True), r=['Him', 'Fim'], w=[('ps', bk)])
                    if d_ == 0:
                        dv(lambda e, g=g, bk=bk: e.tensor_tensor(out=Macc[:, g, :], in0=self.ps[bk][:, 0:128], in1=maskF[:], op=ALU.mult), [('ps', bk)], [('Macc', g)])
                    else:
                        dv(lambda e, g=g, bk=bk: e.tensor_tensor(out=tmpM[:], in0=self.ps[bk][:, 0:128], in1=maskB[:], op=ALU.mult), [('ps', bk)], ['tmpM'])
                        dv(lambda e, g=g: e.tensor_tensor(out=Macc[:, g, :], in0=Macc[:, g, :], in1=tmpM[:], op=ALU.add), [('Macc', g), 'tmpM'], [('Macc', g)])
                    bk2 = 2 + gl % 2
                    S.add('pe', lambda e, gl=gl, bk2=bk2: e.transpose(self.ps[bk2][:, 0:64], tabs['Wre'][:, gl, :], self.ident[0:64, 0:64]), r=['Wre'], w=[('ps', bk2, 0)])
                    S.add('pe', lambda e, gl=gl, bk2=bk2: e.transpose(self.ps[bk2][:, 64:128], tabs['Wim'][:, gl, :], self.ident[0:64, 0:64]), r=['Wim'], w=[('ps', bk2, 1)])
                    S.add('act', lambda e, gl=gl, bk2=bk2, wi=wi: e.copy(out=Wst[wi][:, gl, :, :].rearrange("p a b -> p (a b)"), in_=self.ps[bk2][:, 0:128]),
                          r=[('ps', bk2, 0), ('ps', bk2, 1)], w=[('Wst', wi)])
                S.add('act', lambda e, wi=wi: e.copy(out=Fst[wi][:, :, 0, :], in_=tabs['Fre'][:]), r=['Fre'], w=[('Fst', wi, 0)])
                S.add('act', lambda e, wi=wi: e.copy(out=Fst[wi][:, :, 1, :], in_=tabs['Fim'][:]), r=['Fim'], w=[('Fst', wi, 1)])
                S.add('sp', lambda e, wi=wi, d_=d_, gq=gq: e.dma_start(out=self.Wtab.ap()[:, d_, gq * GQ:(gq + 1) * GQ], in_=Wst[wi][:]), r=[('Wst', wi)], w=[('Wtab', d_, gq)], dma=True)
                S.add('sp', lambda e, wi=wi, d_=d_, gq=gq: e.dma_start(out=self.Ftab.ap()[:, d_, gq * GQ:(gq + 1) * GQ], in_=Fst[wi][:]), r=[('Fst', wi, 0), ('Fst', wi, 1)], w=[('Ftab', d_, gq)], dma=True)
        for g in range(64):
            dv(lambda e, g=g: e.scalar_tensor_tensor(out=Macc[:, g, :], in0=self.ident[:], scalar=dcol[:, g:g + 1], in1=Macc[:, g, :], op0=ALU.mult, op1=ALU.add), [('Macc', g)], [('Macc', g)])
        for hh in range(2):
            S.add('act', lambda e, hh=hh: e.copy(out=Mst[hh][:], in_=Macc[:, hh * 32:(hh + 1) * 32, :]), r=[('Macc', g) for g in range(hh * 32, hh * 32 + 32)], w=[('Mst', hh)])
            S.add('sp', lambda e, hh=hh: e.dma_start(out=self.Mtab.ap()[:, hh * 32:(hh + 1) * 32, :], in_=Mst[hh][:]), r=[('Mst', hh)], w=[('Mtab', hh)], dma=True)
        S.barrier()

    def s5_main(self, li):
        nc, S = self.nc, self.S
        j = li // 2
        L = self.L
        NJ = L // 8
        assert NJ == 512 or True
        A = self.top.sub()
        X = A.tile("X", [128, 64, NJ], BF16)
        selA = A.tile("selA", [128, 8, 8, 128], BF16)
        selB = A.tile("selB", [128, 8, 8, 128], BF16)
        KSc = A.tile("KSc", [128, 2, 32, 27], F32)
        wglu = A.tile("wglu", [128, NDT, D], BF16)
        glub = A.tile("glub", [128, NDT], F32)
        S.add('sp', lambda e: e.dma_start(out=selA[:], in_=self.selA_in.ap()), w=['selA'], dma=True)
        S.add('sp', lambda e: e.dma_start(out=selB[:], in_=self.selB_in.ap()), w=['selB'], dma=True)
        for gi in range(2):
            S.add('sp', lambda e, gi=gi: e.dma_start(out=KSc[gi * 64:(gi + 1) * 64], in_=self.KSd.ap()[:, :, :, gi, :]), w=[('KSc', gi)], dma=True)
        wsrc = self.wglu_s[j].ap().rearrange("(dt p) n -> dt p n", p=128)
        for dt in range(NDT):
            S.add('sp', lambda e, dt=dt: e.dma_start(out=wglu[:, dt, :], in_=wsrc[dt]), r=[('wglu_s', j, dt)], w=[('wglu', dt)], dma=True)
        S.add('sp', lambda e: e.dma_start(out=glub[:], in_=self.s5_glu_b.ap()[j].rearrange("(dt p) -> p dt", p=128), allow_slow_non_contiguous=True), w=['glub'], dma=True)
        S.barrier()
        for s_ in range(self.NSEQ):
            self.s5_A(li, s_, A, X, selA)
            self.s5_B(li, s_, A, X, KSc)
            self.s5_C(li, s_, A, X, selB, wglu, glub)

    def s5_A(self, li, s_, A0, X, selA):
        nc, S = self.nc, self.S
        L = self.L
        A = A0.sub()
        hT = [A.tile("hT", [128, NDT, 512], F32) for _ in range(2)]
        hn = A.tile("hn", [128, NDT, 512], BF16)
        sq = A.tile("sq", [128, NDT, 512], BF16)
        rtmp = A.tile("rtmp", [128, 512], F32)
        rstd = A.tile("rstd", [128, 512], F32)
        eps_t = A.tile("eps", [128, 1], F32)
        S.add('dve', lambda e: e.memset(eps_t[:], NORM_EPS), w=['eps'])
        hsrc = self.hcur.ap().rearrange("(dt p) t -> p dt t", p=128)
        base = s_ * L
        ev = 0
        for b in range(L // 512):
            i = b % 2
            hkeys = [('hT', i, dt) for dt in range(NDT)]
            S.add('sp', lambda e, i=i, b=b: e.dma_start(out=hT[i][:], in_=hsrc[:, :, base + b * 512: base + (b + 1) * 512]), w=hkeys, dma=True)
            self.rmsnorm_fm(hT[i], hkeys, 512, li, sq, [('sq', dt) for dt in range(NDT)], self.ps[0], ('ps', 0), rtmp, rstd,
                            (lambda dt: hn[:, dt, :]), [('hn', dt) for dt in range(NDT)], eps_t)
            for gs in range(8):
                bk = 1 + gs % 4
                for tau in range(8):
                    for dt in range(NDT):
                        S.add('pe', lambda e, dt=dt, gs=gs, tau=tau, bk=bk: e.matmul(self.ps[bk][:, dt * 64:(dt + 1) * 64], lhsT=selA[:, tau, gs, :],
                                                                                   rhs=hn[:, dt, :].rearrange("p (j t) -> p t j", t=8)[:, tau, :], start=(tau == 0 and dt == 0), stop=(tau == 7 and dt == NDT - 1), skip_group_check=True),
                              r=['selA', ('hn', dt)], w=[('ps', bk, dt)])
                ev += 1
                eng = 'act' if ev % 2 == 0 else 'dve'
                dstX = lambda gs=gs, b=b: X[:, :, b * 64:(b + 1) * 64].rearrange("p (dt gs) j -> p gs dt j", gs=8)[:, gs]
                if eng == 'act':
                    S.add('act', lambda e, bk=bk, dstX=dstX: e.copy(out=dstX(), in_=self.ps[bk].rearrange("p (g j) -> p g j", g=8)),
                          r=[('ps', bk, dt) for dt in range(8)], w=[('X', dt * 8 + gs) for dt in range(8)])
                else:
                    S.add('dve', lambda e, bk=bk, dstX=dstX: e.tensor_copy(out=dstX(), in_=self.ps[bk].rearrange("p (g j) -> p g j", g=8)),
                          r=[('ps', bk, dt) for dt in range(8)], w=[('X', dt * 8 + gs) for dt in range(8)])
        S.barrier()

    def s5_B(self, li, s_, A0, X, KSc):
        nc, S = self.nc, self.S
        L = self.L
        NJ = L // 8
        PAD = NJ // 2
        A = A0.sub()
        KB = [[[A.tile("KB", [128, PAD + NJ], F32) for _ in range(2)] for _ in range(2)] for _ in range(2)]
        Sbf = [[A.tile("Sbf", [128, NJ + 1], BF16) for _ in range(2)] for _ in range(2)]
        Wu = [A.tile("Wu", [128, 2, 2, 2, 64], BF16) for _ in range(2)]
        Fu = [A.tile("Fu", [128, 2, 2, 128], BF16) for _ in range(2)]
        Mu = [A.tile("Mu", [128, 2, 128], BF16) for _ in range(2)]
        for d_ in range(2):
            for pp in range(2):
                for part in range(2):
                    S.add('dve', lambda e, d_=d_, pp=pp, part=part: e.memset(KB[d_][pp][part][:], 0.0), w=[('KB', d_, pp, part)])
            for part in range(2):
                S.add('dve', lambda e, d_=d_, part=part: e.memset(Sbf[d_][part][:], 0.0), w=[('Sbf', d_, part)])
        nsteps = int(math.log2(NJ))
        for u in range(32):
            i = u % 2
            S.add('sp', lambda e, i=i, u=u: e.dma_start(out=Wu[i][:], in_=self.Wtab.ap()[:, :, 2 * u:2 * u + 2]), r=[('Wtab', d_, gq) for d_ in range(2) for gq in range(4)], w=[('Wu', i)], dma=True)
            for gi in range(2):
                S.add('sp', lambda e, i=i, u=u, gi=gi: e.dma_start(out=Fu[i][gi * 64:(gi + 1) * 64], in_=self.Ftab.ap()[:, :, 2 * u + gi]), r=[('Ftab', d_, gq) for d_ in range(2) for gq in range(4)], w=[('Fu', i, gi)], dma=True)
            S.add('sp', lambda e, i=i, u=u: e.dma_start(out=Mu[i][:], in_=self.Mtab.ap()[:, 2 * u:2 * u + 2, :]), r=[('Mtab', 0), ('Mtab', 1)], w=[('Mu', i)], dma=True)
            for d_ in range(2):
                for part in range(2):
                    bk = d_ * 2 + part
                    for gi in range(2):
                        g = 2 * u + gi
                        if gi == 0:
                            S.add('pe', lambda e, i=i, d_=d_, part=part, g=g, bk=bk: e.matmul(self.ps[bk][0:64, 0:NJ], lhsT=Wu[i][:, d_, 0, part, :], rhs=X[:, g, :], start=True, stop=True),
                                  r=[('Wu', i), ('X', g)], w=[('ps', bk, 0)])
                        else:
                            S.add('pe', lambda e, i=i, d_=d_, part=part, g=g, bk=bk: e.matmul(self.ps[bk][64:128, 0:NJ], lhsT=Wu[i][:, d_, 1, part, :], rhs=X[:, g, :], start=True, stop=True, tile_position=(0, 64)),
                                  r=[('Wu', i), ('X', g)], w=[('ps', bk, 1)])
                    off = PAD if d_ == 0 else 0
                    S.add('act', lambda e, d_=d_, part=part, bk=bk, off=off: e.copy(out=KB[d_][0][part][:, off:off + NJ], in_=self.ps[bk][:, 0:NJ]),
                          r=[('ps', bk, 0), ('ps', bk, 1)], w=[('KB', d_, 0, part)])
            for k in range(nsteps):
                last = (k == nsteps - 1)
                stage1, stage2 = [], []
                for d_ in range(2):
                    off = PAD if d_ == 0 else 0
                    sft = (1 << k) if d_ == 0 else -(1 << k)
                    src = KB[d_][k % 2]
                    sre, sim = src[0], src[1]
                    cur = slice(off, off + NJ)
                    shf = slice(off - sft, off - sft + NJ)
                    sc = lambda c, u=u, d_=d_, k=k: KSc[:, d_, u, 3 * k + c:3 * k + c + 1]
                    rk = [('KB', d_, k % 2, 0), ('KB', d_, k % 2, 1), ('KSc', 0), ('KSc', 1)]
                    tre = KB[d_][(k + 1) % 2][0][:, off:off + NJ]
                    tim = KB[d_][(k + 1) % 2][1][:, off:off + NJ]
                    tkr, tki = ('KB', d_, (k + 1) % 2, 0), ('KB', d_, (k + 1) % 2, 1)
                    if last:
                        so = 1 if d_ == 0 else 0
                        dre = Sbf[d_][0][:, so:so + NJ]
                        dim = Sbf[d_][1][:, so:so + NJ]
                        kre, kim = ('Sbf', d_, 0), ('Sbf', d_, 1)
                    else:
                        dre, dim, kre, kim = tre, tim, tkr, tki
                    stage1.append((lambda e, sim=sim, sre=sre, shf=shf, cur=cur, tre=tre, sc=sc: e.scalar_tensor_tensor(out=tre, in0=sim[:, shf], scalar=sc(2), in1=sre[:, cur], op0=ALU.mult, op1=ALU.add), rk, [tkr]))
                    stage1.append((lambda e, sim=sim, sre=sre, shf=shf, cur=cur, tim=tim, sc=sc: e.scalar_tensor_tensor(out=tim, in0=sre[:, shf], scalar=sc(1), in1=sim[:, cur], op0=ALU.mult, op1=ALU.add), rk, [tki]))
                    stage2.append((lambda e, sre=sre, shf=shf, tre=tre, dre=dre, sc=sc: e.scalar_tensor_tensor(out=dre, in0=sre[:, shf], scalar=sc(0), in1=tre, op0=ALU.mult, op1=ALU.add), rk + [tkr], [kre]))
                    stage2.append((lambda e, sim=sim, shf=shf, tim=tim, dim=dim, sc=sc: e.scalar_tensor_tensor(out=dim, in0=sim[:, shf], scalar=sc(0), in1=tim, op0=ALU.mult, op1=ALU.add), rk + [tki], [kim]))
                for fn, r_, w_ in stage1 + stage2:
                    S.add('dve', fn, r=r_, w=w_)
            for gi in range(2):
                g = 2 * u + gi
                bk = 4 + (2 * u + gi) % 4
                S.add('pe', lambda e, i=i, gi=gi, g=g, bk=bk: e.matmul(self.ps[bk][:, 0:NJ], lhsT=Mu[i][:, gi, :], rhs=X[:, g, :], start=True, stop=False), r=[('Mu', i), ('X', g)], w=[('ps', bk)])
                n = 0
                for d_ in range(2):
                    so = 0 if d_ == 0 else 1
                    for part in range(2):
                        n += 1
                        S.add('pe', lambda e, i=i, gi=gi, d_=d_, part=part, so=so, bk=bk, n=n: e.matmul(self.ps[bk][:, 0:NJ], lhsT=Fu[i][gi * 64:(gi + 1) * 64, d_, part, :],
                                                                                                  rhs=Sbf[d_][part][gi * 64:(gi + 1) * 64, so:so + NJ], start=False, stop=(n == 4)),
                              r=[('Fu', i, gi), ('Sbf', d_, part)], w=[('ps', bk)])
                S.add('act', lambda e, g=g, bk=bk: e.activation(out=X[:, g, :], in_=self.ps[bk][:, 0:NJ], func=AF.Gelu_apprx_tanh), r=[('ps', bk)], w=[('X', g)])
        S.barrier()

    def s5_C(self, li, s_, A0, X, selB, wglu, glub):
        nc, S = self.nc, self.S
        L = self.L
        A = A0.sub()
        hT = [A.tile("hT", [128, NDT, 512], F32) for _ in range(2)]
        gT = A.tile("gT", [128, NDT, 512], BF16)
        sg = [A.tile("sg", [128, 512], F32) for _ in range(2)]
        hsrc = self.hcur.ap().rearrange("(dt p) t -> p dt t", p=128)
        hdst = self.hnxt.ap().rearrange("(dt p) t -> p dt t", p=128)
        base = s_ * L
        ev = 0
        for b in range(L // 512):
            i = b % 2
            hkeys = [('hT', i, dt) for dt in range(NDT)]
            S.add('sp', lambda e, i=i, b=b: e.dma_start(out=hT[i][:], in_=hsrc[:, :, base + b * 512: base + (b + 1) * 512]), w=hkeys, dma=True)
            for dh in range(2):
                for tau in range(8):
                    for gs in range(8):
                        for dq in range(4):
                            dt = dh * 4 + dq
                            S.add('pe', lambda e, dt=dt, dq=dq, gs=gs, tau=tau, b=b: e.matmul(self.ps[dq].rearrange("p (j t) -> p t j", t=8)[:, tau, :], lhsT=selB[:, tau, gs, :],
                                                                                            rhs=X[:, dt * 8 + gs, b * 64:(b + 1) * 64], start=(gs == 0 and tau == 0), stop=(gs == 7 and tau == 7), skip_group_check=True),
                                  r=['selB', ('X', dt * 8 + gs)], w=[('ps', dq, tau)])
                for dq in range(4):
                    dt = dh * 4 + dq
                    ev += 1
                    if ev % 2 == 0:
                        S.add('act', lambda e, dt=dt, dq=dq: e.copy(out=gT[:, dt, :], in_=self.ps[dq]), r=[('ps', dq, tau) for tau in range(8)], w=[('gT', dt)])
                    else:
                        S.add('dve', lambda e, dt=dt, dq=dq: e.tensor_copy(out=gT[:, dt, :], in_=self.ps[dq]), r=[('ps', dq, tau) for tau in range(8)], w=[('gT', dt)])
            for nt in range(NDT):
                bk = 4 + nt % 4
                k2 = nt % 2
                for dt in range(NDT):
                    S.add('pe', lambda e, nt=nt, dt=dt, bk=bk: e.matmul(self.ps[bk], lhsT=wglu[:, dt, nt * 128:(nt + 1) * 128], rhs=gT[:, dt, :], start=(dt == 0), stop=(dt == NDT - 1)),
                          r=[('wglu', dt), ('gT', dt)], w=[('ps', bk)])
                S.add('act', lambda e, nt=nt, bk=bk, k2=k2: e.activation(out=sg[k2][:], in_=self.ps[bk], func=AF.Sigmoid, bias=glub[:, nt:nt + 1]), r=[('ps', bk), 'glub'], w=[('sg', k2)])
                S.add('dve', lambda e, nt=nt, k2=k2: e.tensor_tensor(out=sg[k2][:], in0=sg[k2][:], in1=gT[:, nt, :], op=ALU.mult), r=[('sg', k2), ('gT', nt)], w=[('sg', k2)])
                S.add('dve', lambda e, nt=nt, k2=k2, i=i: e.tensor_tensor(out=hT[i][:, nt, :], in0=hT[i][:, nt, :], in1=sg[k2][:], op=ALU.add), r=[('sg', k2), ('hT', i, nt)], w=[('hT', i, nt)])
            S.add('sp', lambda e, i=i, b=b: e.dma_start(out=hdst[:, :, base + b * 512: base + (b + 1) * 512], in_=hT[i][:]), r=hkeys, w=[('h3', s_, b)], dma=True)
        S.barrier()

    def phase_out(self):
        nc, S = self.nc, self.S
        depth = self.cfg['depth']
        A = self.top.sub()
        hT = [A.tile("hT", [128, NDT, 512], F32) for _ in range(2)]
        hn = [A.tile("hnf", [128, NDT, 512], F32) for _ in range(2)]
        sq = A.tile("sq", [128, NDT, 512], BF16)
        rtmp = A.tile("rtmp", [128, 512], F32)
        rstd = A.tile("rstd", [128, 512], F32)
        ot = [A.tile("ot", [128, 4, D], F32) for _ in range(2)]
        eps_t = A.tile("eps", [128, 1], F32)
        S.add('dve', lambda e: e.memset(eps_t[:], NORM_EPS), w=['eps'])
        hsrc = self.hcur.ap().rearrange("(dt p) t -> p dt t", p=128)
        ov = self.out.ap().rearrange("(b t p) d -> b p t d", p=128, t=4)
        nblk = self.NTOK // 512
        for b in range(nblk):
            i = b % 2
            hkeys = [('hT', i, dt) for dt in range(NDT)]
            S.add('sp', lambda e, i=i, b=b: e.dma_start(out=hT[i][:], in_=hsrc[:, :, b * 512:(b + 1) * 512]), w=hkeys, dma=True)
            self.rmsnorm_fm(hT[i], hkeys, 512, 2 * depth, sq, [('sq', dt) for dt in range(NDT)], self.ps[0], ('ps', 0), rtmp, rstd,
                            (lambda dt, i=i: hn[i][:, dt, :]), [('hnf', i, dt) for dt in range(NDT)], eps_t)
            for t in range(4):
                for half in range(2):
                    bank = self.ps[1 + (t * 2 + half) % 4]
                    bkey = ('ps', 1 + (t * 2 + half) % 4)
                    for q in range(4):
                        dt = half * 4 + q
                        S.add('pe', lambda e, i=i, dt=dt, t=t, q=q, bank=bank: e.transpose(bank[:, q * 128:(q + 1) * 128], hn[i][:, dt, t * 128:(t + 1) * 128], self.ident[:]),
                              r=[('hnf', i, dt), 'ident'], w=[(bkey, q)])
                    if half == 0:
                        S.add('act', lambda e, i=i, t=t, half=half, bank=bank: e.copy(out=ot[i][:, t, half * 512:(half + 1) * 512], in_=bank[:]),
                              r=[(bkey, q) for q in range(4)], w=[('ot', i, t, half)])
                    else:
                        S.add('dve', lambda e, i=i, t=t, half=half, bank=bank: e.tensor_copy(out=ot[i][:, t, half * 512:(half + 1) * 512], in_=bank[:]),
                              r=[(bkey, q) for q in range(4)], w=[('ot', i, t, half)])
            S.add('sp', lambda e, i=i, b=b: e.dma_start(out=ov[b], in_=ot[i][:]), r=[('ot', i, t, h) for t in range(4) for h in range(2)], w=[('out', b)], dma=True)


def make_cfg(nseq=2, L=4096, depth=4, layers=None):
    if layers is None:
        layers = []
        for i in range(depth):
            layers.append(('s5' if i % 2 == 0 else 'attn', i))
            layers.append(('ffn', i))
    return dict(nseq=nseq, L=L, depth=depth, nA=(depth + 1) // 2, nB=depth // 2, layers=layers,
                ffn_layers=[li for (k, li) in layers if k == 'ffn'])


def _rel_bucket_np(rel):
    nb = 16
    ret = np.where(rel > 0, nb, 0)
    n = np.abs(rel)
    max_exact = nb // 2
    nf = np.maximum(n, 1).astype(np.float32)
    large = max_exact + (np.log(nf / np.float32(max_exact)) / np.float32(math.log(128 / max_exact)) * np.float32(nb - max_exact)).astype(np.int32)
    large = np.minimum(large, nb - 1)
    return ret + np.where(n < max_exact, n, large)


def host_consts(rel_bias=None, s5=True):
    c = {"ident": np.eye(128, dtype=np.float32)}
    if s5:
        selA = np.zeros((128, 8, 8, 128), dtype=np.float32)
        for tau in range(8):
            for gs in range(8):
                for cc in range(16):
                    selA[gs * 16 + cc, tau, gs, tau * 16 + cc] = 1.0
        selB = np.ascontiguousarray(selA.transpose(3, 1, 2, 0))
        c["selA"] = selA.astype(ml_dtypes.bfloat16)
        c["selB"] = selB.astype(ml_dtypes.bfloat16)
        sig = np.arange(128)[:, None] // 16
        tau = np.arange(128)[None, :] // 16
        c["maskF"] = (sig <= tau).astype(np.float32)
        c["maskB"] = (sig >= tau).astype(np.float32)
    if rel_bias is not None:
        rb = np.asarray(rel_bias, dtype=np.float32)
        p = np.arange(128)[:, None]
        xx = np.arange(1152)[None, :]
        idx = _rel_bucket_np(p - xx + 512)
        c["reltab"] = np.ascontiguousarray(rb[idx].transpose(2, 0, 1))
        far = np.stack([rb[15], rb[31]], axis=1).reshape(1, 16)
        c["relfar"] = np.ascontiguousarray(np.broadcast_to(far, (128, 16)))
    return c


_CACHE = {}


def kernel(**inputs):
    ncores = 8
    cfg = make_cfg()
    if 'nc' not in _CACHE:
        _CACHE['nc'] = K(cfg).build()
    nc = _CACHE['nc']
    x = np.ascontiguousarray(inputs['x'], dtype=np.float32)
    B, L, _ = x.shape
    per = B // ncores
    consts = host_consts(inputs['rel_bias'], s5=True)
    shared = {}
    for k, v in inputs.items():
        if k in ('x', 'rel_bias'):
            continue
        shared[k] = np.ascontiguousarray(v, dtype=np.float32)
    in_maps = []
    for c in range(ncores):
        m = {"x": x[c * per:(c + 1) * per].reshape(per * L, D)}
        m.update(shared)
        m.update(consts)
        in_maps.append(m)
    res = run_bass_kernel_spmd(nc, in_maps, core_ids=list(range(ncores)))
    outs = [r["out"].reshape(per, L, D) for r in res.results]
    return np.concatenate(outs, axis=0)
```

```python
import math
from contextlib import ExitStack
import numpy as np
import ml_dtypes
import concourse.bass as bass
import concourse.mybir as mybir
from concourse.bass_utils import run_bass_kernel_spmd

F32 = mybir.dt.float32
BF16 = mybir.dt.bfloat16
I32 = mybir.dt.int32
ALU = mybir.AluOpType
AF = mybir.ActivationFunctionType

D = 1024
NDT = 8
DFF = 2816
NFI = 22
NORM_EPS = 1e-6
SUBLN_EPS = 1e-5
SB_BASE = 16640
SB_END = 229376

SAME_ENGINE_SYNC = True
NDMA_SLOTS = 6


class Op:
    __slots__ = ('eng', 'fn', 'deps', 'dma', 'need_inc', 'sem', 'val', 'prev_slot')


class Sched:
    ENGS = ['pe', 'act', 'dve', 'pool', 'sp']
    BENGS = ['pe', 'act', 'dve', 'sp']

    def __init__(self):
        self.q = {e: [] for e in self.ENGS}
        self.st = {}
        self.nops = 0

    def add(self, eng, fn, r=(), w=(), dma=False):
        o = Op()
        o.eng = eng; o.fn = fn; o.dma = dma; o.need_inc = False
        o.sem = None; o.val = 0; o.prev_slot = None
        deps = set()
        st = self.st
        for k in r:
            s = st.get(k)
            if s is not None and s[0] is not None:
                deps.add(s[0])
        for k in w:
            s = st.get(k)
            if s is not None:
                if s[0] is not None:
                    deps.add(s[0])
                deps.update(s[1])
        for k in r:
            s = st.get(k)
            if s is None:
                st[k] = [None, [o]]
            else:
                s[1].append(o)
        for k in w:
            st[k] = [o, []]
        deps.discard(o)
        o.deps = deps
        self.q[eng].append(o)
        self.nops += 1
        return o

    KEEP = ('wup_s', 'wdn_s', 'cvf', 'cvb', 'wscr', 'wqkv_s', 'wo_s', 'wglu_s')

    def barrier(self):
        lasts = []
        for e in self.BENGS:
            cnt = 0
            gotc = False
            for o in reversed(self.q[e]):
                if o.dma:
                    if cnt < NDMA_SLOTS:
                        lasts.append(o); cnt += 1
                elif not gotc and o.fn is not None:
                    lasts.append(o); gotc = True
                if gotc and cnt >= NDMA_SLOTS:
                    break
        for e in self.BENGS:
            o = Op()
            o.eng = e; o.fn = None; o.dma = False; o.need_inc = False
            o.sem = None; o.val = 0; o.prev_slot = None
            o.deps = set(lasts)
            self.q[e].append(o)
        self.st = {k: v for k, v in self.st.items() if isinstance(k, tuple) and k[0] in self.KEEP}

    def emit(self, nc, stack):
        csem = {}
        for e in ['pe', 'act', 'dve', 'pool']:
            csem[e] = stack.enter_context(nc.semaphore('s_' + e))
        dsem = {}
        for e in self.ENGS:
            if any(o.dma for o in self.q[e]):
                dsem[e] = [stack.enter_context(nc.semaphore('d_%s%d' % (e, i))) for i in range(NDMA_SLOTS)]

        def skip(d, o):
            return (not d.dma) and (not o.dma) and d.eng == o.eng and (d.eng == 'pe' or not SAME_ENGINE_SYNC)

        for e in self.ENGS:
            for o in self.q[e]:
                for d in o.deps:
                    if d.dma or skip(d, o):
                        continue
                    d.need_inc = True
        finals = {}
        for e in self.ENGS:
            cnt = 0
            di = 0
            uses = [0] * NDMA_SLOTS
            lastop = [None] * NDMA_SLOTS
            for o in self.q[e]:
                if o.dma:
                    sl = di % NDMA_SLOTS
                    di += 1
                    uses[sl] += 1
                    o.sem = dsem[e][sl]; o.val = 16 * uses[sl]
                    o.prev_slot = lastop[sl]
                    lastop[sl] = o
                elif o.need_inc:
                    cnt += 1
                    o.sem = csem[e]; o.val = cnt
            finals[e] = [x for x in lastop if x is not None]
        sched = self

        def run(e, eng):
            seen = {}

            def wait(d):
                key = id(d.sem)
                if seen.get(key, 0) >= d.val:
                    return
                seen[key] = d.val
                eng.wait_ge(d.sem, d.val)

            for o in sched.q[e]:
                for d in o.deps:
                    if skip(d, o):
                        continue
                    wait(d)
                if o.dma and o.prev_slot is not None:
                    wait(o.prev_slot)
                if o.fn is None:
                    continue
                ins = o.fn(eng)
                if o.dma:
                    ins.then_inc(o.sem, 16)
                elif o.need_inc:
                    ins.then_inc(o.sem, 1)
            for o in finals[e]:
                wait(o)

        block = stack.enter_context(nc.Block())

        @block.tensor
        def _(eng):
            run('pe', eng)

        @block.scalar
        def _(eng):
            run('act', eng)

        @block.vector
        def _(eng):
            run('dve', eng)

        @block.gpsimd
        def _(eng):
            run('pool', eng)

        @block.sync
        def _(eng):
            run('sp', eng)


class Arena:
    def __init__(self, nc, base, end):
        self.nc = nc; self.base = base; self.end = end; self.off = base; self.n = 0

    def tile(self, name, shape, dtype):
        esz = 2 if str(dtype) == str(BF16) else 4
        nbytes = esz
        for s in shape[1:]:
            nbytes *= s
        off = (self.off + 31) // 32 * 32
        assert off + nbytes <= self.end, ("SBUF overflow", name, off + nbytes, self.end)
        self.n += 1
        t = self.nc.alloc_sbuf_tensor_at("%s_%d_%d" % (name, self.base, self.n), list(shape), dtype, offset=off)
        self.off = off + nbytes
        return t

    def sub(self):
        return Arena(self.nc, (self.off + 31) // 32 * 32, self.end)


class K:
    def __init__(self, cfg):
        self.cfg = cfg
        self.NSEQ = cfg['nseq']
        self.L = cfg['L']
        self.NTOK = self.NSEQ * self.L
        self.layers = cfg['layers']
        self.nc = bass.Bass("TRN2", target_bir_lowering=False)
        self.S = Sched()
        self.dram = {}

    def din(self, name, shape, dtype=F32):
        t = self.nc.dram_tensor(name, list(shape), dtype, kind="ExternalInput")
        self.dram[name] = t
        return t

    def dscr(self, name, shape, dtype):
        t = self.nc.dram_tensor(name, list(shape), dtype)
        self.dram[name] = t
        return t

    def build(self):
        nc, S, cfg = self.nc, self.S, self.cfg
        NTOK = self.NTOK
        depth = cfg['depth']
        nA, nB = cfg['nA'], cfg['nB']
        self.x = self.din("x", [NTOK, D])
        self.out = nc.dram_tensor("out", [NTOK, D], F32, kind="ExternalOutput")
        self.norm_mix_g = self.din("norm_mix_g", [depth, D])
        self.norm_ffn_g = self.din("norm_ffn_g", [depth, D])
        self.final_norm_g = self.din("final_norm_g", [D])
        if cfg['ffn_layers']:
            self.ffn_w_up = self.din("ffn_w_up", [depth, D, 2 * DFF])
            self.ffn_conv_w = self.din("ffn_conv_w", [depth, 3, 2 * DFF])
            self.ffn_conv_b = self.din("ffn_conv_b", [depth, 2 * DFF])
            self.ffn_w_down = self.din("ffn_w_down", [depth, DFF, D])
        self.ident_in = self.din("ident", [128, 128])
        if any(k == 's5' for k, _ in self.layers):
            self.s5_lre = self.din("s5_lambda_re", [nA, 2, 64, 64])
            self.s5_lim = self.din("s5_lambda_im", [nA, 2, 64, 64])
            self.s5_lst = self.din("s5_log_step", [nA, 2, 64])
            self.s5_bre = self.din("s5_b_re", [nA, 2, 64, 64, 16])
            self.s5_bim = self.din("s5_b_im", [nA, 2, 64, 64, 16])
            self.s5_cre = self.din("s5_c_re", [nA, 2, 64, 16, 64])
            self.s5_cim = self.din("s5_c_im", [nA, 2, 64, 16, 64])
            self.s5_d = self.din("s5_d", [nA, D])
            self.s5_glu_w = self.din("s5_glu_w", [nA, D, D])
            self.s5_glu_b = self.din("s5_glu_b", [nA, D])
            self.selA_in = self.din("selA", [128, 8, 8, 128], BF16)
            self.selB_in = self.din("selB", [128, 8, 8, 128], BF16)
            self.maskF_in = self.din("maskF", [128, 128])
            self.maskB_in = self.din("maskB", [128, 128])
            self.wglu_s = [self.dscr("wglu_s%d" % i, [D, D], BF16) for i in range(nA)]
            self.Wtab = self.dscr("Wtab", [128, 2, 64, 2, 64], BF16)
            self.Ftab = self.dscr("Ftab", [64, 2, 64, 2, 128], BF16)
            self.Mtab = self.dscr("Mtab", [128, 64, 128], BF16)
            self.KSd = self.dscr("KSd", [64, 2, 32, 2, 27], F32)
        if any(k == 'attn' for k, _ in self.layers):
            self.attn_w_qkv = self.din("attn_w_qkv", [nB, D, 3 * D])
            self.attn_w_o = self.din("attn_w_o", [nB, D, D])
            self.attn_lq1 = self.din("attn_lambda_q1", [nB, 64])
            self.attn_lk1 = self.din("attn_lambda_k1", [nB, 64])
            self.attn_lq2 = self.din("attn_lambda_q2", [nB, 64])
            self.attn_lk2 = self.din("attn_lambda_k2", [nB, 64])
            self.attn_subln_g = self.din("attn_subln_g", [nB, 128])
            self.reltab = self.din("reltab", [8, 128, 1152])
            self.relfar = self.din("relfar", [128, 16])
            self.wqkv_s = [self.dscr("wqkv_s%d" % i, [D, 3 * D], BF16) for i in range(nB)]
            self.wo_s = [self.dscr("wo_s%d" % i, [D, D], BF16) for i in range(nB)]
            self.QT = self.dscr("QT", [D, NTOK], BF16)
            self.KT = self.dscr("KT", [D, NTOK], BF16)
            self.Vs = self.dscr("Vs", [NTOK, D], BF16)
            self.AT = self.dscr("AT", [D, NTOK], BF16)
        self.hA = self.dscr("hA", [D, NTOK], F32)
        self.hB = self.dscr("hB", [D, NTOK], F32)
        self.hcur, self.hnxt = self.hA, self.hB
        if cfg['ffn_layers']:
            self.wup_s = [self.dscr("wup_s%d" % i, [D, 2 * DFF], BF16) for i in range(depth)]
            self.wdn_s = [self.dscr("wdn_s%d" % i, [NDT, 128, NFI, 128], BF16) for i in range(depth)]

        with ExitStack() as st:
            self.st = st
            self.psall = st.enter_context(nc.psum_tensor("psall", [128, 8, 512], F32))
            self.ps = [self.psall[:, i, :] for i in range(8)]
            top = Arena(nc, SB_BASE, SB_END)
            self.ident = top.tile("ident", [128, 128], F32)
            self.ones_bf = top.tile("ones", [128, 128], BF16)
            self.cv_f = [top.tile("cvf", [128, 1024], F32) for _ in range(2)]
            self.cv_b = [top.tile("cvb", [128, 1024], BF16) for _ in range(2)]
            self.gains = top.tile("gains", [128, 2 * depth + 1, NDT], F32)
            self.top = top
            S.add('sp', lambda e: e.dma_start(out=self.ident[:], in_=self.ident_in.ap()), w=['ident'], dma=True)
            S.add('dve', lambda e: e.memset(self.ones_bf[:], 1.0), w=['ones'])
            S.add('sp', lambda e: e.dma_start(out=self.gains[:, 0:depth, :], in_=self.norm_mix_g.ap().rearrange("l (dt p) -> p l dt", p=128), allow_slow_non_contiguous=True), w=['gains0'], dma=True)
            S.add('sp', lambda e: e.dma_start(out=self.gains[:, depth:2 * depth, :], in_=self.norm_ffn_g.ap().rearrange("l (dt p) -> p l dt", p=128), allow_slow_non_contiguous=True), w=['gains1'], dma=True)
            S.add('sp', lambda e: e.dma_start(out=self.gains[:, 2 * depth, :], in_=self.final_norm_g.ap().rearrange("(dt p) -> p dt", p=128), allow_slow_non_contiguous=True), w=['gains2'], dma=True)
            self.cvn = 0
            for (kind, li) in self.layers:
                if kind == 'ffn':
                    self.convert_ffn_weights(li)
                elif kind == 'attn':
                    self.convert_attn_weights(li // 2)
                elif kind == 's5':
                    self.convert_s5_weights(li // 2)
            self.phase_in()
            for (kind, li) in self.layers:
                if kind == 'ffn':
                    self.phase_ffn(li)
                elif kind == 'attn':
                    self.phase_attn(li)
                elif kind == 's5':
                    self.phase_s5(li)
            self.phase_out()
            S.emit(nc, st)
        return nc

    def convert_ffn_weights(self, li):
        S = self.S
        wup = self.ffn_w_up.ap()[li].rearrange("(dt p) n -> dt p n", p=128)
        dst = self.wup_s[li].ap().rearrange("(dt p) n -> dt p n", p=128)
        for dt in range(NDT):
            for c0 in range(0, 2 * DFF, 1024):
                nc_ = min(1024, 2 * DFF - c0)
                S_src = (lambda dt=dt, c0=c0, nc_=nc_: wup[dt, :, c0:c0 + nc_])
                self._conv_chunk(S_src, (lambda b, dt=dt, c0=c0, nc_=nc_: dst[dt, :, c0:c0 + nc_]), nc_, ('wup_s', li, dt, c0))
        wdn = self.ffn_w_down.ap()[li].rearrange("(fi p) n -> fi p n", p=128)
        dstd = self.wdn_s[li].ap()
        for fi in range(NFI):
            self._conv_chunk((lambda fi=fi: wdn[fi]),
                             (lambda b, fi=fi: dstd[:, :, fi, :].rearrange("dt p c -> p dt c")), 1024, ('wdn_s', li, fi),
                             src_view=(lambda b: b[:, 0:1024].rearrange("p (dt c) -> p dt c", dt=NDT)))

    def _conv_chunk(self, src_fn, dst_fn, ncols, dst_key, src_view=None):
        S = self.S
        i = self.cvn % 2
        self.cvn += 1
        f, b = self.cv_f[i], self.cv_b[i]
        S.add('pool', lambda e: e.dma_start(out=f[:, 0:ncols], in_=src_fn()), w=[('cvf', i)], dma=True)
        S.add('pool', lambda e: e.tensor_copy(out=b[:, 0:ncols], in_=f[:, 0:ncols]), r=[('cvf', i)], w=[('cvb', i)])
        if src_view is None:
            S.add('pool', lambda e: e.dma_start(out=dst_fn(b), in_=b[:, 0:ncols]), r=[('cvb', i)], w=[dst_key], dma=True)
        else:
            S.add('pool', lambda e: e.dma_start(out=dst_fn(b), in_=src_view(b)), r=[('cvb', i)], w=[dst_key], dma=True)

    def phase_in(self):
        nc, S = self.nc, self.S
        A = self.top.sub()
        xin = [A.tile("xin", [128, 4, D], F32) for _ in range(2)]
        stg = [A.tile("stg", [128, NDT, 512], F32) for _ in range(2)]
        xv = self.x.ap().rearrange("(b t p) d -> b p t d", p=128, t=4)
        hv = self.hcur.ap().rearrange("(dt p) t -> p dt t", p=128)
        nblk = self.NTOK // 512
        for b in range(nblk):
            i = b % 2
            S.add('sp', lambda e, b=b, i=i: e.dma_start(out=xin[i][:], in_=xv[b]), w=[('xin', i)], dma=True)
            for dt in range(NDT):
                bank = self.ps[dt % 4]
                for t in range(4):
                    S.add('pe', lambda e, i=i, dt=dt, t=t, bank=bank: e.transpose(bank[:, t * 128:(t + 1) * 128], xin[i][:, t, dt * 128:(dt + 1) * 128], self.ident[:]),
                          r=[('xin', i), 'ident'], w=[('psb', dt % 4, t)])
                eng = 'act' if dt % 2 == 0 else 'dve'
                if eng == 'act':
                    S.add('act', lambda e, i=i, dt=dt, bank=bank: e.copy(out=stg[i][:, dt, :], in_=bank[:]),
                          r=[('psb', dt % 4, t) for t in range(4)], w=[('stg', i, dt)])
                else:
                    S.add('dve', lambda e, i=i, dt=dt, bank=bank: e.tensor_copy(out=stg[i][:, dt, :], in_=bank[:]),
                          r=[('psb', dt % 4, t) for t in range(4)], w=[('stg', i, dt)])
            S.add('sp', lambda e, b=b, i=i: e.dma_start(out=hv[:, :, b * 512:(b + 1) * 512], in_=stg[i][:]),
                  r=[('stg', i, dt) for dt in range(NDT)], w=[('h', id(self.hcur), b)], dma=True)
        S.barrier()

    def rmsnorm_fm(self, hT, hT_keys, W2, gain_idx, sq, sq_keys, ps_bank, ps_key, rtmp, rstd, out_fn, out_keys, eps_t, out_eng='dve'):
        S = self.S
        S.add('act', lambda e: e.activation(out=sq[:, :, 0:W2], in_=hT[:, :, 0:W2], func=AF.Square), r=hT_keys, w=sq_keys)
        for dt in range(NDT):
            S.add('pe', lambda e, dt=dt: e.matmul(ps_bank[:, 0:W2], lhsT=self.ones_bf[:], rhs=sq[:, dt, 0:W2], start=(dt == 0), stop=(dt == NDT - 1)),
                  r=[sq_keys[dt], 'ones'], w=[ps_key])
        S.add('act', lambda e: e.activation(out=rtmp[:, 0:W2], in_=ps_bank[:, 0:W2], func=AF.Sqrt, scale=1.0 / D, bias=eps_t[:, 0:1]), r=[ps_key, 'eps'], w=['rtmp'])
        S.add('dve', lambda e: e.reciprocal(out=rstd[:, 0:W2], in_=rtmp[:, 0:W2]), r=['rtmp'], w=['rstd'])
        for dt in range(NDT):
            S.add(out_eng, lambda e, dt=dt: e.scalar_tensor_tensor(out=out_fn(dt), in0=hT[:, dt, 0:W2], scalar=self.gains[:, gain_idx, dt:dt + 1],
                                                                   in1=rstd[:, 0:W2], op0=ALU.mult, op1=ALU.mult),
                  r=[hT_keys[dt], 'rstd', 'gains0', 'gains1', 'gains2'], w=[out_keys[dt]])

    def phase_ffn(self, li):
        nc, S = self.nc, self.S
        depth = self.cfg['depth']
        A = self.top.sub()
        wup = A.tile("wup", [128, NDT, 2 * DFF], BF16)
        wdn = [A.tile("wdn", [128, NFI, 128], BF16) for _ in range(3)]
        hT = [A.tile("hT", [128, NDT, 512], F32) for _ in range(2)]
        hn = A.tile("hn", [128, NDT, 512], BF16)
        hid = A.tile("hid", [128, NFI, 512], BF16)
        sq = hid
        cg = [A.tile("cg", [128, 512], F32) for _ in range(2)]
        cvt = [A.tile("cvt", [128, 512], F32) for _ in range(2)]
        rtmp = A.tile("rtmp", [128, 512], F32)
        rstd = A.tile("rstd", [128, 512], F32)
        cw = A.tile("cw", [128, 3, 2 * NFI], F32)
        cb = A.tile("cb", [128, 2 * NFI], F32)
        eps_t = A.tile("eps", [128, 1], F32)
        S.add('dve', lambda e: e.memset(eps_t[:], NORM_EPS), w=['eps'])
        for k3 in range(3):
            S.add('sp', lambda e, k3=k3: e.dma_start(out=cw[:, k3, :], in_=self.ffn_conv_w.ap()[li, k3].rearrange("(fc p) -> p fc", p=128), allow_slow_non_contiguous=True), w=[('cw', k3)], dma=True)
        S.add('sp', lambda e: e.dma_start(out=cb[:], in_=self.ffn_conv_b.ap()[li].rearrange("(fc p) -> p fc", p=128), allow_slow_non_contiguous=True), w=['cb'], dma=True)
        wsrc = self.wup_s[li].ap().rearrange("(dt p) n -> dt p n", p=128)
        for dt in range(NDT):
            S.add('sp', lambda e, dt=dt: e.dma_start(out=wup[:, dt, :], in_=wsrc[dt]),
                  r=[('wup_s', li, dt, c0) for c0 in range(0, 2 * DFF, 1024)], w=[('wup', dt)], dma=True)
        hsrc = self.hcur.ap().rearrange("(dt p) t -> p dt t", p=128)
        hdst = self.hnxt.ap().rearrange("(dt p) t -> p dt t", p=128)
        wdsrc = self.wdn_s[li].ap()
        L = self.L
        blocks = []
        for s in range(self.NSEQ):
            t0 = 0
            while t0 < L:
                w = min(510, L - t0)
                blocks.append((s, t0, w))
                t0 += w
        gi = depth + li
        wdn_i = 0
        sqv = hid[:, 0:NDT, :]
        for bi, (s, t0, w) in enumerate(blocks):
            i = bi % 2
            W2 = w + 2
            base = s * L
            lo = 1 if t0 == 0 else 0
            hi = W2 - 1 if t0 + w == L else W2
            hkeys = [('hT', i, dt) for dt in range(NDT)]
            S.add('sp', lambda e, i=i, lo=lo, hi=hi, base=base, t0=t0: e.dma_start(out=hT[i][:, :, lo:hi], in_=hsrc[:, :, base + t0 - 1 + lo: base + t0 - 1 + hi]),
                  w=hkeys, dma=True)
            if lo == 1:
                S.add('dve', lambda e, i=i: e.memset(hT[i][:, :, 0:1], 0.0), w=hkeys)
            if hi == W2 - 1:
                S.add('dve', lambda e, i=i, W2=W2: e.memset(hT[i][:, :, W2 - 1:W2], 0.0), w=hkeys)
            self.rmsnorm_fm(hT[i], hkeys, W2, gi, sqv, [('hid', dt) for dt in range(NDT)], self.ps[0], ('ps', 0), rtmp, rstd,
                            (lambda dt, W2=W2: hn[:, dt, 0:W2]), [('hn', dt) for dt in range(NDT)], eps_t)
            for fi in range(NFI):
                j = fi % 2
                pg, pv = self.ps[1 + 2 * j], self.ps[2 + 2 * j]
                for half, pbank, pkey in ((0, pg, ('ps', 1 + 2 * j)), (1, pv, ('ps', 2 + 2 * j))):
                    c0 = (half * NFI + fi) * 128
                    for dt in range(NDT):
                        S.add('pe', lambda e, dt=dt, c0=c0, pbank=pbank, W2=W2: e.matmul(pbank[:, 0:W2], lhsT=wup[:, dt, c0:c0 + 128], rhs=hn[:, dt, 0:W2], start=(dt == 0), stop=(dt == NDT - 1)),
                              r=[('wup', dt), ('hn', dt)], w=[pkey])
                halves = ((0, pg, ('ps', 1 + 2 * j), cg[j], ('cg', j)), (1, pv, ('ps', 2 + 2 * j), cvt[j], ('cvt', j)))
                for half, pbank, pkey, dstt, dkey in halves:
                    fc = half * NFI + fi
                    S.add('act', lambda e, pbank=pbank, dstt=dstt, fc=fc, w=w: e.activation(out=dstt[:, 0:w], in_=pbank[:, 1:w + 1], func=AF.Identity, scale=cw[:, 1, fc:fc + 1], bias=cb[:, fc:fc + 1]),
                          r=[pkey, ('cw', 1), 'cb'], w=[dkey])
                for tap, c_lo in ((0, 0), (2, 2)):
                    for half, pbank, pkey, dstt, dkey in halves:
                        fc = half * NFI + fi
                        S.add('dve', lambda e, pbank=pbank, dstt=dstt, fc=fc, w=w, tap=tap, c_lo=c_lo: e.scalar_tensor_tensor(out=dstt[:, 0:w], in0=pbank[:, c_lo:c_lo + w], scalar=cw[:, tap, fc:fc + 1], in1=dstt[:, 0:w], op0=ALU.mult, op1=ALU.add),
                              r=[pkey, ('cw', tap), dkey], w=[dkey])
                S.add('act', lambda e, j=j, w=w: e.activation(out=cg[j][:, 0:w], in_=cg[j][:, 0:w], func=AF.Silu), r=[('cg', j)], w=[('cg', j)])
                S.add('dve', lambda e, j=j, fi=fi, w=w: e.tensor_tensor(out=hid[:, fi, 0:w], in0=cg[j][:, 0:w], in1=cvt[j][:, 0:w], op=ALU.mult),
                      r=[('cg', j), ('cvt', j)], w=[('hid', fi)])
            for dt in range(NDT):
                k = wdn_i % 3
                wdn_i += 1
                S.add('sp', lambda e, k=k, dt=dt: e.dma_start(out=wdn[k][:], in_=wdsrc[dt]), r=[('wdn_s', li, fi) for fi in range(NFI)], w=[('wdn', k)], dma=True)
                pbank = self.ps[5 + dt % 2]
                pkey = ('ps', 5 + dt % 2)
                for fi in range(NFI):
                    S.add('pe', lambda e, k=k, fi=fi, pbank=pbank, w=w: e.matmul(pbank[:, 0:w], lhsT=wdn[k][:, fi, :], rhs=hid[:, fi, 0:w], start=(fi == 0), stop=(fi == NFI - 1)),
                          r=[('wdn', k), ('hid', fi)], w=[pkey])
                S.add('dve', lambda e, i=i, dt=dt, pbank=pbank, w=w: e.tensor_tensor(out=hT[i][:, dt, 1:w + 1], in0=pbank[:, 0:w], in1=hT[i][:, dt, 1:w + 1], op=ALU.add),
                      r=[pkey, ('hT', i, dt)], w=[('hT', i, dt)])
            S.add('sp', lambda e, i=i, base=base, t0=t0, w=w: e.dma_start(out=hdst[:, :, base + t0: base + t0 + w], in_=hT[i][:, :, 1:w + 1]),
                  r=hkeys, w=[('h', id(self.hnxt), bi)], dma=True)
        S.barrier()
        self.hcur, self.hnxt = self.hnxt, self.hcur

    def convert_attn_weights(self, j):
        wq = self.attn_w_qkv.ap()[j].rearrange("(dt p) n -> dt p n", p=128)
        dq = self.wqkv_s[j].ap().rearrange("(dt p) n -> dt p n", p=128)
        for dt in range(NDT):
            for c0 in range(0, 3 * D, 1024):
                self._conv_chunk((lambda dt=dt, c0=c0: wq[dt, :, c0:c0 + 1024]), (lambda b, dt=dt, c0=c0: dq[dt, :, c0:c0 + 1024]), 1024, ('wqkv_s', j, dt, c0))
        wo = self.attn_w_o.ap()[j].rearrange("(dt p) n -> dt p n", p=128)
        do = self.wo_s[j].ap().rearrange("(dt p) n -> dt p n", p=128)
        for dt in range(NDT):
            self._conv_chunk((lambda dt=dt: wo[dt]), (lambda b, dt=dt: do[dt]), 1024, ('wo_s', j, dt))

    def phase_attn(self, li):
        self.phase_attn_a(li)
        self.phase_attn_b(li)
        self.phase_attn_c(li)
        self.hcur, self.hnxt = self.hnxt, self.hcur

    def phase_attn_a(self, li):
        nc, S = self.nc, self.S
        j = li // 2
        L = self.L
        NTOK = self.NTOK
        nblk = NTOK // 512
        lambda_init = 0.8 - 0.6 * math.exp(-0.3 * li)
        A = self.top.sub()
        wqkv = A.tile("wqkv", [128, NDT, 3 * D], BF16)
        hT = [A.tile("hT", [128, NDT, 512], F32) for _ in range(2)]
        hn = A.tile("hn", [128, NDT, 512], BF16)
        sq = A.tile("sq", [128, NDT, 512], BF16)
        rtmp = A.tile("rtmp", [128, 512], F32)
        rstd = A.tile("rstd", [128, 512], F32)
        qstg = [A.tile("qstg", [128, 16, 512], BF16) for _ in range(2)]
        vstg = [A.tile("vstg", [128, 4, D], BF16) for _ in range(2)]
        eps_t = A.tile("eps", [128, 1], F32)
        S.add('dve', lambda e: e.memset(eps_t[:], NORM_EPS), w=['eps'])
        wsrc = self.wqkv_s[j].ap().rearrange("(dt p) n -> dt p n", p=128)
        for dt in range(NDT):
            S.add('sp', lambda e, dt=dt: e.dma_start(out=wqkv[:, dt, :], in_=wsrc[dt]),
                  r=[('wqkv_s', j, dt, c0) for c0 in range(0, 3 * D, 1024)], w=[('wqkv', dt)], dma=True)
        hsrc = self.hcur.ap().rearrange("(dt p) t -> p dt t", p=128)
        qdst = self.QT.ap().rearrange("(nt p) t -> p nt t", p=128)
        kdst = self.KT.ap().rearrange("(nt p) t -> p nt t", p=128)
        vdst = self.Vs.ap().rearrange("(b tt p) n -> b p tt n", p=128, tt=4)
        ev = 0
        for b in range(nblk):
            i = b % 2
            hkeys = [('hT', i, dt) for dt in range(NDT)]
            S.add('sp', lambda e, i=i, b=b: e.dma_start(out=hT[i][:], in_=hsrc[:, :, b * 512:(b + 1) * 512]), w=hkeys, dma=True)
            self.rmsnorm_fm(hT[i], hkeys, 512, li, sq, [('sq', dt) for dt in range(NDT)], self.ps[0], ('ps', 0), rtmp, rstd,
                            (lambda dt: hn[:, dt, :]), [('hn', dt) for dt in range(NDT)], eps_t)
            for nt in range(16):
                bk = 1 + nt % 4
                for dt in range(NDT):
                    S.add('pe', lambda e, nt=nt, dt=dt, bk=bk: e.matmul(self.ps[bk], lhsT=wqkv[:, dt, nt * 128:(nt + 1) * 128], rhs=hn[:, dt, :], start=(dt == 0), stop=(dt == NDT - 1)),
                          r=[('wqkv', dt), ('hn', dt)], w=[('ps', bk)])
                ev += 1
                if ev % 2 == 0:
                    S.add('act', lambda e, i=i, nt=nt, bk=bk: e.copy(out=qstg[i][:, nt, :], in_=self.ps[bk]), r=[('ps', bk)], w=[('qstg', i, nt)])
                else:
                    S.add('dve', lambda e, i=i, nt=nt, bk=bk: e.tensor_copy(out=qstg[i][:, nt, :], in_=self.ps[bk]), r=[('ps', bk)], w=[('qstg', i, nt)])
            S.add('sp', lambda e, i=i, b=b: e.dma_start(out=qdst[:, :, b * 512:(b + 1) * 512], in_=qstg[i][:, 0:8, :]), r=[('qstg', i, nt) for nt in range(8)], w=[('QT', b)], dma=True)
            S.add('sp', lambda e, i=i, b=b: e.dma_start(out=kdst[:, :, b * 512:(b + 1) * 512], in_=qstg[i][:, 8:16, :]), r=[('qstg', i, nt) for nt in range(8, 16)], w=[('KT', b)], dma=True)
            for tt in range(4):
                for half in range(2):
                    bk = 5 + (tt * 2 + half) % 3
                    for dt in range(NDT):
                        S.add('pe', lambda e, tt=tt, half=half, dt=dt, bk=bk: e.matmul(self.ps[bk], lhsT=hn[:, dt, tt * 128:(tt + 1) * 128], rhs=wqkv[:, dt, 2 * D + half * 512: 2 * D + (half + 1) * 512], start=(dt == 0), stop=(dt == NDT - 1)),
                              r=[('wqkv', dt), ('hn', dt)], w=[('ps', bk)])
                    ev += 1
                    if ev % 2 == 0:
                        S.add('act', lambda e, i=i, tt=tt, half=half, bk=bk: e.copy(out=vstg[i][:, tt, half * 512:(half + 1) * 512], in_=self.ps[bk]), r=[('ps', bk)], w=[('vstg', i, tt, half)])
                    else:
                        S.add('dve', lambda e, i=i, tt=tt, half=half, bk=bk: e.tensor_copy(out=vstg[i][:, tt, half * 512:(half + 1) * 512], in_=self.ps[bk]), r=[('ps', bk)], w=[('vstg', i, tt, half)])
            S.add('sp', lambda e, i=i, b=b: e.dma_start(out=vdst[b], in_=vstg[i][:]), r=[('vstg', i, tt, h) for tt in range(4) for h in range(2)], w=[('Vs', b)], dma=True)
        S.barrier()

    def phase_attn_b(self, li):
        nc, S = self.nc, self.S
        j = li // 2
        L = self.L
        NTOK = self.NTOK
        nblk = NTOK // 512
        lambda_init = 0.8 - 0.6 * math.exp(-0.3 * li)
        A = self.top.sub()
        KTh = [A.tile("KTh", [128, L], BF16) for _ in range(2)]
        QTh = [A.tile("QTh", [128, L], BF16) for _ in range(2)]
        Vh = [A.tile("Vh", [128, L // 128, 128], BF16) for _ in range(2)]
        tab = [A.tile("tab", [128, 1152], F32) for _ in range(2)]
        far = A.tile("far", [128, 16], F32)
        PT = [A.tile("PT", [128, 2, 512], BF16) for _ in range(3)]
        tmpb = [A.tile("tmpb", [128, 2, 512], F32) for _ in range(2)]
        r1 = A.tile("r1", [128, 512], F32)
        ev_o1 = A.tile("ev_o1", [128, 512], F32)
        ev_o2 = A.tile("ev_o2", [128, 512], F32)
        ev_l1 = A.tile("ev_l1", [128, 512], F32)
        ev_l2 = A.tile("ev_l2", [128, 512], F32)
        r2 = A.tile("r2", [128, 512], F32)
        oa = A.tile("oa", [128, 512], F32)
        ob = A.tile("ob", [128, 512], F32)
        osq = A.tile("osq", [128, 512], BF16)
        lnt = A.tile("lnt", [128, 512], F32)
        rs2 = A.tile("rs2", [128, 512], F32)
        aout = [A.tile("aout", [128, 512], BF16) for _ in range(2)]
        lqk = A.tile("lqk", [64, 4], F32)
        lprod = A.tile("lprod", [64, 2], F32)
        ones_f = A.tile("ones_f", [64, 128], F32)
        lam_t = A.tile("lam_t", [128, 4], F32)
        gsub = A.tile("gsub", [128, 1], F32)
        eps2 = A.tile("eps2", [128, 1], F32)
        for c, src in enumerate((self.attn_lq1, self.attn_lk1, self.attn_lq2, self.attn_lk2)):
            S.add('sp', lambda e, c=c, src=src: e.dma_start(out=lqk[:, c:c + 1], in_=src.ap()[j].rearrange("(p o) -> p o", o=1)), w=[('lqk', c)], dma=True)
        S.add('sp', lambda e: e.dma_start(out=gsub[:], in_=self.attn_subln_g.ap()[j].rearrange("(p o) -> p o", o=1)), w=['gsub'], dma=True)
        S.add('sp', lambda e: e.dma_start(out=far[:], in_=self.relfar.ap()), w=['far'], dma=True)
        S.add('dve', lambda e: e.memset(ones_f[:], 1.0), w=['ones_f'])
        S.add('dve', lambda e: e.memset(eps2[:], SUBLN_EPS), w=['eps2'])
        S.add('dve', lambda e: e.tensor_tensor(out=lprod[:, 0:1], in0=lqk[:, 0:1], in1=lqk[:, 1:2], op=ALU.mult), r=[('lqk', 0), ('lqk', 1)], w=[('lprod', 0)])
        S.add('dve', lambda e: e.tensor_tensor(out=lprod[:, 1:2], in0=lqk[:, 2:3], in1=lqk[:, 3:4], op=ALU.mult), r=[('lqk', 2), ('lqk', 3)], w=[('lprod', 1)])
        S.add('pe', lambda e: e.matmul(self.ps[0][:, 0:2], lhsT=ones_f[:], rhs=lprod[:], start=True, stop=True), r=['ones_f', ('lprod', 0), ('lprod', 1)], w=[('ps', 0)])
        S.add('act', lambda e: e.activation(out=lam_t[:, 0:2], in_=self.ps[0][:, 0:2], func=AF.Exp), r=[('ps', 0)], w=['lam_e'])
        S.add('dve', lambda e: e.tensor_tensor(out=lam_t[:, 2:3], in0=lam_t[:, 1:2], in1=lam_t[:, 0:1], op=ALU.subtract), r=['lam_e'], w=['lam_d'])
        S.add('dve', lambda e: e.tensor_scalar(out=lam_t[:, 3:4], in0=lam_t[:, 2:3], scalar1=-lambda_init, scalar2=None, op0=ALU.add), r=['lam_d'], w=['neglam'])
        S.add('dve', lambda e: e.tensor_scalar(out=gsub[:], in0=gsub[:], scalar1=(1.0 - lambda_init), scalar2=None, op0=ALU.mult), r=['gsub'], w=['gsub'])
        NKT = L // 128
        NQB = L // 512
        atd = self.AT.ap().rearrange("(h p) t -> h p t", p=128)
        heads = [(s_, h) for s_ in range(self.NSEQ) for h in range(8)]
        its = []
        for hi, (s_, h) in enumerate(heads):
            for qb in range(NQB):
                for kt in range(NKT):
                    its.append((hi, s_, h, qb, kt))

        def emit_head_load(hi):
            s_, h = heads[hi]
            i = hi % 2
            base = s_ * L
            S.add('sp', lambda e: e.dma_start(out=KTh[i][:], in_=self.KT.ap()[h * 128:(h + 1) * 128, base:base + L]),
                  r=[('KT', b) for b in range(nblk)], w=[('KTh', i)], dma=True)
            S.add('sp', lambda e: e.dma_start(out=QTh[i][:], in_=self.QT.ap()[h * 128:(h + 1) * 128, base:base + L]),
                  r=[('QT', b) for b in range(nblk)], w=[('QTh', i)], dma=True)
            S.add('sp', lambda e: e.dma_start(out=Vh[i][:], in_=self.Vs.ap()[base:base + L, h * 128:(h + 1) * 128].rearrange("(kt p) e -> p kt e", p=128)),
                  r=[('Vs', b) for b in range(nblk)], w=[('Vh', i)], dma=True)
            S.add('sp', lambda e: e.dma_start(out=tab[i][:], in_=self.reltab.ap()[h]), w=[('tab', i)], dma=True)

        def emit_scores(idx):
            hi, s_, h, qb, kt = its[idx]
            i = hi % 2
            sb = 2 * (idx % 2)
            for t in range(2):
                S.add('pe', lambda e, t=t: e.matmul(self.ps[sb + t], lhsT=KTh[i][t * 64:(t + 1) * 64, kt * 128:(kt + 1) * 128], rhs=QTh[i][t * 64:(t + 1) * 64, qb * 512:(qb + 1) * 512], start=True, stop=True),
                      r=[('KTh', i), ('QTh', i)], w=[('ps', sb + t)])

        tbi = [0]

        def emit_exp(idx):
            hi, s_, h, qb, kt = its[idx]
            i = hi % 2
            sb = 2 * (idx % 2)
            p_i = idx % 3
            delta = kt * 128 - qb * 512
            if -128 <= delta <= 512:
                tb = tbi[0] % 2
                tbi[0] += 1
                x0 = 512 - delta
                S.add('dve', lambda e: e.scalar_tensor_tensor(out=tmpb[tb][:], in0=self.psall[:, sb:sb + 2, :], scalar=0.125,
                                                             in1=tab[i][:, x0:x0 + 512].unsqueeze(1).to_broadcast([128, 2, 512]), op0=ALU.mult, op1=ALU.add),
                      r=[('ps', sb), ('ps', sb + 1), ('tab', i)], w=[('tmpb', tb)])
                S.add('act', lambda e: e.activation(out=PT[p_i][:], in_=tmpb[tb][:], func=AF.Exp), r=[('tmpb', tb)], w=[('PT', p_i)])
            else:
                side = 0 if delta < 0 else 1
                S.add('act', lambda e: e.activation(out=PT[p_i][:], in_=self.psall[:, sb:sb + 2, :], func=AF.Exp, scale=0.125, bias=far[:, h * 2 + side:h * 2 + side + 1]),
                      r=[('ps', sb), ('ps', sb + 1), 'far'], w=[('PT', p_i)])

        def emit_pv(idx):
            hi, s_, h, qb, kt = its[idx]
            i = hi % 2
            p_i = idx % 3
            for t in range(2):
                S.add('pe', lambda e, t=t: e.matmul(self.ps[4 + t], lhsT=Vh[i][:, kt, :], rhs=PT[p_i][:, t, :], start=(kt == 0), stop=(kt == NKT - 1)),
                      r=[('Vh', i), ('PT', p_i)], w=[('ps', 4 + t)])
            for t in range(2):
                S.add('pe', lambda e, t=t: e.matmul(self.ps[6 + t], lhsT=self.ones_bf[:], rhs=PT[p_i][:, t, :], start=(kt == 0), stop=(kt == NKT - 1)),
                      r=[('PT', p_i)], w=[('ps', 6 + t)])

        aoi = [0]

        def emit_epilogue(idx):
            hi, s_, h, qb, kt = its[idx]
            base = s_ * L
            ao = aoi[0] % 2
            aoi[0] += 1
            S.add('dve', lambda e: e.tensor_copy(out=ev_o1[:], in_=self.ps[4]), r=[('ps', 4)], w=['ev_o1'])
            S.add('dve', lambda e: e.tensor_copy(out=ev_o2[:], in_=self.ps[5]), r=[('ps', 5)], w=['ev_o2'])
            S.add('dve', lambda e: e.tensor_copy(out=ev_l1[:], in_=self.ps[6]), r=[('ps', 6)], w=['ev_l1'])
            S.add('dve', lambda e: e.tensor_copy(out=ev_l2[:], in_=self.ps[7]), r=[('ps', 7)], w=['ev_l2'])
            S.add('dve', lambda e: e.reciprocal(out=r1[:], in_=ev_l1[:]), r=['ev_l1'], w=['r1'])
            S.add('dve', lambda e: e.tensor_tensor(out=oa[:], in0=ev_o1[:], in1=r1[:], op=ALU.mult), r=['ev_o1', 'r1'], w=['oa'])
            S.add('dve', lambda e: e.reciprocal(out=r2[:], in_=ev_l2[:]), r=['ev_l2'], w=['r2'])
            S.add('dve', lambda e: e.scalar_tensor_tensor(out=ob[:], in0=ev_o2[:], scalar=lam_t[:, 3:4], in1=r2[:], op0=ALU.mult, op1=ALU.mult), r=['ev_o2', 'r2', 'neglam'], w=['ob'])
            S.add('dve', lambda e: e.tensor_tensor(out=aout[ao][:], in0=oa[:], in1=ob[:], op=ALU.add), r=['oa', 'ob'], w=[('aout', ao)])
            S.add('sp', lambda e: e.dma_start(out=atd[h, :, base + qb * 512: base + (qb + 1) * 512], in_=aout[ao][:]), r=[('aout', ao)], w=[('AT', s_, h, qb)], dma=True)

        emit_head_load(0)
        loaded = 0
        for idx in range(len(its)):
            hi = its[idx][0]
            if idx == 0:
                emit_scores(0)
            if (idx == 0 or its[idx - 1][0] != hi) and hi + 1 < len(heads) and loaded < hi + 1:
                emit_head_load(hi + 1)
                loaded = hi + 1
            if idx + 1 < len(its):
                emit_scores(idx + 1)
            emit_exp(idx)
            emit_pv(idx)
            if its[idx][4] == NKT - 1:
                emit_epilogue(idx)
        S.barrier()

    def phase_attn_c(self, li):
        nc, S = self.nc, self.S
        j = li // 2
        L = self.L
        NTOK = self.NTOK
        nblk = NTOK // 512
        lambda_init = 0.8 - 0.6 * math.exp(-0.3 * li)
        A = self.top.sub()
        wo = A.tile("wo", [128, NDT, D], BF16)
        hT = [A.tile("hT", [128, NDT, 512], F32) for _ in range(2)]
        at = [A.tile("at", [128, NDT, 512], BF16) for _ in range(2)]
        atn = A.tile("atn", [128, NDT, 512], BF16)
        sqt = A.tile("sqt", [128, NDT, 512], BF16)
        lnt = [A.tile("lnt", [128, 512], F32) for _ in range(2)]
        rs2 = [A.tile("rs2", [128, 512], F32) for _ in range(2)]
        gsub = A.tile("gsub", [128, 1], F32)
        eps2 = A.tile("eps2", [128, 1], F32)
        S.add('sp', lambda e: e.dma_start(out=gsub[:], in_=self.attn_subln_g.ap()[j].rearrange("(p o) -> p o", o=1)), w=['gsub'], dma=True)
        S.add('dve', lambda e: e.memset(eps2[:], SUBLN_EPS), w=['eps2'])
        S.add('dve', lambda e: e.tensor_scalar(out=gsub[:], in0=gsub[:], scalar1=(1.0 - lambda_init), scalar2=None, op0=ALU.mult), r=['gsub'], w=['gsub'])
        wsrc = self.wo_s[j].ap().rearrange("(dt p) n -> dt p n", p=128)
        for dt in range(NDT):
            S.add('sp', lambda e, dt=dt: e.dma_start(out=wo[:, dt, :], in_=wsrc[dt]), r=[('wo_s', j, dt)], w=[('wo', dt)], dma=True)
        hsrc = self.hcur.ap().rearrange("(dt p) t -> p dt t", p=128)
        hdst = self.hnxt.ap().rearrange("(dt p) t -> p dt t", p=128)
        asrc = self.AT.ap().rearrange("(h p) t -> p h t", p=128)
        for b in range(nblk):
            i = b % 2
            hkeys = [('hT', i, dt) for dt in range(NDT)]
            S.add('sp', lambda e, i=i, b=b: e.dma_start(out=hT[i][:], in_=hsrc[:, :, b * 512:(b + 1) * 512]), w=hkeys, dma=True)
            S.add('sp', lambda e, i=i, b=b: e.dma_start(out=at[i][:], in_=asrc[:, :, b * 512:(b + 1) * 512]), w=[('at', i)], dma=True)
            S.add('act', lambda e, i=i: e.activation(out=sqt[:], in_=at[i][:], func=AF.Square), r=[('at', i)], w=['sqt'])
            for hh in range(NDT):
                bk = 4 + hh % 4
                k2 = hh % 2
                S.add('pe', lambda e, hh=hh, bk=bk: e.matmul(self.ps[bk], lhsT=self.ones_bf[:], rhs=sqt[:, hh, :], start=True, stop=True), r=['sqt'], w=[('ps', bk)])
                S.add('act', lambda e, bk=bk, k2=k2: e.activation(out=lnt[k2][:], in_=self.ps[bk], func=AF.Ln, scale=1.0 / 128, bias=eps2[:, 0:1]), r=[('ps', bk), 'eps2'], w=[('lnt', k2)])
                S.add('act', lambda e, k2=k2: e.activation(out=rs2[k2][:], in_=lnt[k2][:], func=AF.Exp, scale=-0.5), r=[('lnt', k2)], w=[('rs2', k2)])
                S.add('dve', lambda e, i=i, hh=hh, k2=k2: e.scalar_tensor_tensor(out=atn[:, hh, :], in0=at[i][:, hh, :], scalar=gsub[:, 0:1], in1=rs2[k2][:], op0=ALU.mult, op1=ALU.mult),
                      r=[('at', i), 'gsub', ('rs2', k2)], w=[('atn', hh)])
            for nt in range(NDT):
                bk = nt % 4
                for hh in range(NDT):
                    S.add('pe', lambda e, i=i, nt=nt, hh=hh, bk=bk: e.matmul(self.ps[bk], lhsT=wo[:, hh, nt * 128:(nt + 1) * 128], rhs=atn[:, hh, :], start=(hh == 0), stop=(hh == NDT - 1)),
                          r=[('wo', hh), ('atn', hh)], w=[('ps', bk)])
                S.add('dve', lambda e, i=i, nt=nt, bk=bk: e.tensor_tensor(out=hT[i][:, nt, :], in0=self.ps[bk], in1=hT[i][:, nt, :], op=ALU.add), r=[('ps', bk), ('hT', i, nt)], w=[('hT', i, nt)])
            S.add('sp', lambda e, i=i, b=b: e.dma_start(out=hdst[:, :, b * 512:(b + 1) * 512], in_=hT[i][:]), r=hkeys, w=[('h2', b)], dma=True)
        S.barrier()

    def convert_s5_weights(self, j):
        wg = self.s5_glu_w.ap()[j].rearrange("(dt p) n -> dt p n", p=128)
        dg = self.wglu_s[j].ap().rearrange("(dt p) n -> dt p n", p=128)
        for dt in range(NDT):
            self._conv_chunk((lambda dt=dt: wg[dt]), (lambda b, dt=dt: dg[dt]), 1024, ('wglu_s', j, dt))

    def phase_s5(self, li):
        self.s5_prep1(li)
        self.s5_prep2(li)
        self.s5_main(li)
        self.hcur, self.hnxt = self.hnxt, self.hcur

    def s5_prep1(self, li):
        nc, S = self.nc, self.S
        j = li // 2
        P = self.top.sub()
        self.s5P = P
        T = {}
        self.s5T = T
        ap_re = P.tile("ap_re", [64, 9, 128], F32); ap_im = P.tile("ap_im", [64, 9, 128], F32)
        ai_re = P.tile("ai_re", [64, 9, 128], F32); ai_im = P.tile("ai_im", [64, 9, 128], F32)
        Bre = P.tile("Bre", [64, 128, 16], F32); Bim = P.tile("Bim", [64, 128, 16], F32)
        CTre = P.tile("CTre", [64, 128, 16], F32); CTim = P.tile("CTim", [64, 128, 16], F32)
        maskF = P.tile("maskF", [128, 128], F32); maskB = P.tile("maskB", [128, 128], F32)
        dcol = P.tile("dcol", [128, 64], F32)
        T.update(ap_re=ap_re, ap_im=ap_im, ai_re=ai_re, ai_im=ai_im, Bre=Bre, Bim=Bim, CTre=CTre, CTim=CTim, maskF=maskF, maskB=maskB, dcol=dcol)
        Q = P.sub()
        lre = Q.tile("lre", [64, 128], F32); lim = Q.tile("lim", [64, 128], F32); lst = Q.tile("lst", [64, 128], F32)
        t = [Q.tile("t%d" % k, [64, 128], F32) for k in range(8)]
        ti = Q.tile("ti", [64, 128], I32)
        fr = Q.tile("fr", [64, 128], F32); fi = Q.tile("fi", [64, 128], F32)
        tA = Q.tile("tA", [64, 128, 16], F32); tB = Q.tile("tB", [64, 128, 16], F32)
        Cld = [Q.tile("Cld", [128, 16, 64], F32) for _ in range(2)]
        kst = Q.tile("kst", [64, 128, 27], F32)
        kr = Q.tile("kr", [64, 128], F32); ki = Q.tile("ki", [64, 128], F32)
        zero_b = Q.tile("zerob", [64, 1], F32)
        dv = lambda fn, r, w: S.add('dve', fn, r=r, w=w)
        S.add('sp', lambda e: e.dma_start(out=lre[:], in_=self.s5_lre.ap()[j].rearrange("d g p -> p (d g)"), allow_slow_non_contiguous=True), w=['lre'], dma=True)
        S.add('sp', lambda e: e.dma_start(out=lim[:], in_=self.s5_lim.ap()[j].rearrange("d g p -> p (d g)"), allow_slow_non_contiguous=True), w=['lim'], dma=True)
        S.add('sp', lambda e: e.dma_start(out=lst[:], in_=self.s5_lst.ap()[j].rearrange("d g -> (d g)").partition_broadcast(64)), w=['lst'], dma=True)
        S.add('sp', lambda e: e.dma_start(out=Bre[:], in_=self.s5_bre.ap()[j].rearrange("d g p c -> p (d g) c")), w=['Bre'], dma=True)
        S.add('sp', lambda e: e.dma_start(out=Bim[:], in_=self.s5_bim.ap()[j].rearrange("d g p c -> p (d g) c")), w=['Bim'], dma=True)
        for d_ in range(2):
            S.add('sp', lambda e, d_=d_: e.dma_start(out=Cld[0][:, d_ * 8:(d_ + 1) * 8, :], in_=self.s5_cre.ap()[j, d_].rearrange("(go gs) c p -> (gs c) go p", gs=8)), w=[('Cld', 0, d_)], dma=True)
            S.add('sp', lambda e, d_=d_: e.dma_start(out=Cld[1][:, d_ * 8:(d_ + 1) * 8, :], in_=self.s5_cim.ap()[j, d_].rearrange("(go gs) c p -> (gs c) go p", gs=8)), w=[('Cld', 1, d_)], dma=True)
        S.add('sp', lambda e: e.dma_start(out=maskF[:], in_=self.maskF_in.ap()), w=['maskF'], dma=True)
        S.add('sp', lambda e: e.dma_start(out=maskB[:], in_=self.maskB_in.ap()), w=['maskB'], dma=True)
        for tau in range(8):
            S.add('sp', lambda e, tau=tau: e.dma_start(out=dcol[tau * 16:(tau + 1) * 16, :], in_=self.s5_d.ap()[j].rearrange("(g c) -> c g", c=16), allow_slow_non_contiguous=True), w=[('dcol', tau)], dma=True)
        dv(lambda e: e.memset(zero_b[:], 0.0), [], ['zerob'])
        dv(lambda e: e.tensor_scalar(out=lre[:], in0=lre[:], scalar1=-1e-4, scalar2=None, op0=ALU.min), ['lre'], ['lre'])
        S.add('act', lambda e: e.activation(out=lst[:], in_=lst[:], func=AF.Exp), r=['lst'], w=['lst'])
        dv(lambda e: e.tensor_tensor(out=t[0][:], in0=lst[:], in1=lre[:], op=ALU.mult), ['lst', 'lre'], ['t0'])
        dv(lambda e: e.tensor_tensor(out=t[1][:], in0=lst[:], in1=lim[:], op=ALU.mult), ['lst', 'lim'], ['t1'])
        S.add('act', lambda e: e.activation(out=t[2][:], in_=t[0][:], func=AF.Exp), r=['t0'], w=['t2'])
        S.add('act', lambda e: e.activation(out=t[7][:], in_=t[0][:], func=AF.Exp, scale=-1.0), r=['t0'], w=['t7'])
        TWO_PI = 2.0 * math.pi
        for which, off, dst in ((0, 0.0, 3), (1, 0.5 * math.pi, 4)):
            dv(lambda e, off=off: e.tensor_scalar(out=t[5][:], in0=t[1][:], scalar1=off, scalar2=1.0 / TWO_PI, op0=ALU.add, op1=ALU.mult), ['t1'], ['t5'])
            dv(lambda e: e.tensor_copy(out=ti[:], in_=t[5][:]), ['t5'], ['ti'])
            dv(lambda e: e.tensor_copy(out=t[5][:], in_=ti[:]), ['ti'], ['t5'])
            dv(lambda e, off=off: e.tensor_scalar(out=t[6][:], in0=t[1][:], scalar1=off, scalar2=None, op0=ALU.add), ['t1'], ['t6'])
            dv(lambda e: e.scalar_tensor_tensor(out=t[6][:], in0=t[5][:], scalar=-TWO_PI, in1=t[6][:], op0=ALU.mult, op1=ALU.add), ['t5', 't6'], ['t6'])
            S.add('act', lambda e, dst=dst: e.activation(out=t[dst][:], in_=t[6][:], func=AF.Sin, bias=zero_b[:, 0:1]), r=['t6', 'zerob'], w=['t%d' % dst])
        dv(lambda e: e.memset(ap_re[:, 0, :], 1.0), [], [('ap', 0)])
        dv(lambda e: e.memset(ap_im[:, 0, :], 0.0), [], [('api', 0)])
        dv(lambda e: e.memset(ai_re[:, 0, :], 1.0), [], [('ai', 0)])
        dv(lambda e: e.memset(ai_im[:, 0, :], 0.0), [], [('aii', 0)])
        dv(lambda e: e.tensor_tensor(out=ap_re[:, 1, :], in0=t[2][:], in1=t[4][:], op=ALU.mult), ['t2', 't4'], [('ap', 1)])
        dv(lambda e: e.tensor_tensor(out=ap_im[:, 1, :], in0=t[2][:], in1=t[3][:], op=ALU.mult), ['t2', 't3'], [('api', 1)])
        dv(lambda e: e.tensor_tensor(out=ai_re[:, 1, :], in0=t[7][:], in1=t[4][:], op=ALU.mult), ['t7', 't4'], [('ai', 1)])
        dv(lambda e: e.scalar_tensor_tensor(out=ai_im[:, 1, :], in0=t[7][:], scalar=-1.0, in1=t[3][:], op0=ALU.mult, op1=ALU.mult), ['t7', 't3'], [('aii', 1)])
        for (pr, pi, kr_, ki_) in ((ap_re, ap_im, 'ap', 'api'), (ai_re, ai_im, 'ai', 'aii')):
            for k in range(2, 9):
                dv(lambda e, pr=pr, pi=pi, k=k: e.tensor_tensor(out=t[5][:], in0=pr[:, k - 1, :], in1=pr[:, 1, :], op=ALU.mult), [(kr_, k - 1), (kr_, 1)], ['t5'])
                dv(lambda e, pr=pr, pi=pi, k=k: e.tensor_tensor(out=t[6][:], in0=pi[:, k - 1, :], in1=pi[:, 1, :], op=ALU.mult), [(ki_, k - 1), (ki_, 1)], ['t6'])
                dv(lambda e, pr=pr, k=k: e.tensor_tensor(out=pr[:, k, :], in0=t[5][:], in1=t[6][:], op=ALU.subtract), ['t5', 't6'], [(kr_, k)])
                dv(lambda e, pr=pr, pi=pi, k=k: e.tensor_tensor(out=t[5][:], in0=pr[:, k - 1, :], in1=pi[:, 1, :], op=ALU.mult), [(kr_, k - 1), (ki_, 1)], ['t5'])
                dv(lambda e, pr=pr, pi=pi, k=k: e.tensor_tensor(out=t[6][:], in0=pi[:, k - 1, :], in1=pr[:, 1, :], op=ALU.mult), [(ki_, k - 1), (kr_, 1)], ['t6'])
                dv(lambda e, pi=pi, k=k: e.tensor_tensor(out=pi[:, k, :], in0=t[5][:], in1=t[6][:], op=ALU.add), ['t5', 't6'], [(ki_, k)])
        dv(lambda e: e.tensor_scalar(out=t[0][:], in0=ap_re[:, 1, :], scalar1=-1.0, scalar2=None, op0=ALU.add), [('ap', 1)], ['t0'])
        dv(lambda e: e.tensor_tensor(out=t[5][:], in0=lre[:], in1=lre[:], op=ALU.mult), ['lre'], ['t5'])
        dv(lambda e: e.tensor_tensor(out=t[6][:], in0=lim[:], in1=lim[:], op=ALU.mult), ['lim'], ['t6'])
        dv(lambda e: e.tensor_tensor(out=t[5][:], in0=t[5][:], in1=t[6][:], op=ALU.add), ['t5', 't6'], ['t5'])
        dv(lambda e: e.reciprocal(out=t[7][:], in_=t[5][:]), ['t5'], ['t7'])
        dv(lambda e: e.tensor_tensor(out=t[5][:], in0=t[0][:], in1=lre[:], op=ALU.mult), ['t0', 'lre'], ['t5'])
        dv(lambda e: e.tensor_tensor(out=t[6][:], in0=ap_im[:, 1, :], in1=lim[:], op=ALU.mult), [('api', 1), 'lim'], ['t6'])
        dv(lambda e: e.tensor_tensor(out=t[5][:], in0=t[5][:], in1=t[6][:], op=ALU.add), ['t5', 't6'], ['t5'])
        dv(lambda e: e.tensor_tensor(out=fr[:], in0=t[5][:], in1=t[7][:], op=ALU.mult), ['t5', 't7'], ['fr'])
        dv(lambda e: e.tensor_tensor(out=t[5][:], in0=ap_im[:, 1, :], in1=lre[:], op=ALU.mult), [('api', 1), 'lre'], ['t5'])
        dv(lambda e: e.tensor_tensor(out=t[6][:], in0=t[0][:], in1=lim[:], op=ALU.mult), ['t0', 'lim'], ['t6'])
        dv(lambda e: e.tensor_tensor(out=t[5][:], in0=t[5][:], in1=t[6][:], op=ALU.subtract), ['t5', 't6'], ['t5'])
        dv(lambda e: e.tensor_tensor(out=fi[:], in0=t[5][:], in1=t[7][:], op=ALU.mult), ['t5', 't7'], ['fi'])
        frb = lambda: fr[:].unsqueeze(2).to_broadcast([64, 128, 16])
        fib = lambda: fi[:].unsqueeze(2).to_broadcast([64, 128, 16])
        dv(lambda e: e.tensor_tensor(out=tA[:], in0=Bim[:], in1=fib(), op=ALU.mult), ['Bim', 'fi'], ['tA'])
        dv(lambda e: e.tensor_tensor(out=tB[:], in0=Bre[:], in1=fib(), op=ALU.mult), ['Bre', 'fi'], ['tB'])
        dv(lambda e: e.tensor_tensor(out=Bre[:], in0=Bre[:], in1=frb(), op=ALU.mult), ['Bre', 'fr', 'tB'], ['Bre'])
        dv(lambda e: e.tensor_tensor(out=Bre[:], in0=Bre[:], in1=tA[:], op=ALU.subtract), ['Bre', 'tA'], ['Bre'])
        dv(lambda e: e.tensor_tensor(out=Bim[:], in0=Bim[:], in1=frb(), op=ALU.mult), ['Bim', 'fr', 'tA'], ['Bim'])
        dv(lambda e: e.tensor_tensor(out=Bim[:], in0=Bim[:], in1=tB[:], op=ALU.add), ['Bim', 'tB'], ['Bim'])
        for ri, (cl, ct, nm) in enumerate(((Cld[0], CTre, 'CTre'), (Cld[1], CTim, 'CTim'))):
            for q in range(4):
                bk = (ri * 4 + q) % 4
                for r_ in range(4):
                    idx = q * 4 + r_
                    S.add('pe', lambda e, cl=cl, idx=idx, bk=bk, r_=r_: e.transpose(self.ps[bk][0:64, r_ * 128:(r_ + 1) * 128], cl[:, idx, :], self.ident[:]),
                          r=[('Cld', ri, idx // 8), 'ident'], w=[('ps', bk, r_)])
                S.add('act', lambda e, ct=ct, q=q, bk=bk: e.copy(out=ct[:, q * 32:(q + 1) * 32, :].rearrange("p g c -> p (g c)"), in_=self.ps[bk][0:64, :]),
                      r=[('ps', bk, r_) for r_ in range(4)], w=[(nm, q)])
        dv(lambda e: e.tensor_copy(out=kr[:], in_=ap_re[:, 8, :]), [('ap', 8)], ['kr'])
        dv(lambda e: e.tensor_copy(out=ki[:], in_=ap_im[:, 8, :]), [('api', 8)], ['ki'])
        for k in range(9):
            dv(lambda e, k=k: e.tensor_copy(out=kst[:, :, 3 * k], in_=kr[:]), ['kr'], [('kst', k, 0)])
            dv(lambda e, k=k: e.tensor_copy(out=kst[:, :, 3 * k + 1], in_=ki[:]), ['ki'], [('kst', k, 1)])
            dv(lambda e, k=k: e.tensor_scalar(out=kst[:, :, 3 * k + 2], in0=ki[:], scalar1=-1.0, scalar2=None, op0=ALU.mult), ['ki'], [('kst', k, 2)])
            if k < 8:
                dv(lambda e: e.tensor_tensor(out=t[5][:], in0=kr[:], in1=kr[:], op=ALU.mult), ['kr'], ['t5'])
                dv(lambda e: e.tensor_tensor(out=t[6][:], in0=ki[:], in1=ki[:], op=ALU.mult), ['ki'], ['t6'])
                dv(lambda e: e.tensor_tensor(out=t[0][:], in0=kr[:], in1=ki[:], op=ALU.mult), ['kr', 'ki'], ['t0'])
                dv(lambda e: e.tensor_tensor(out=kr[:], in0=t[5][:], in1=t[6][:], op=ALU.subtract), ['t5', 't6'], ['kr'])
                dv(lambda e: e.tensor_scalar(out=ki[:], in0=t[0][:], scalar1=2.0, scalar2=None, op0=ALU.mult), ['t0'], ['ki'])
        S.add('sp', lambda e: e.dma_start(out=self.KSd.ap().rearrange("p d u gi k -> p d (u gi) k"), in_=kst[:].rearrange("p (d g) k -> p d g k", d=2)),
              r=[('kst', k, c) for k in range(9) for c in range(3)], w=[('KSd',)], dma=True)
        S.barrier()

    def s5_prep2(self, li):
        nc, S = self.nc, self.S
        j = li // 2
        T = self.s5T
        ap_re, ap_im, ai_re, ai_im = T['ap_re'], T['ap_im'], T['ai_re'], T['ai_im']
        Bre, Bim, CTre, CTim = T['Bre'], T['Bim'], T['CTre'], T['CTim']
        maskF, maskB, dcol = T['maskF'], T['maskB'], T['dcol']
        Q = self.s5P.sub()
        GQ = 16
        tabs = {nm: Q.tile(nm, [64, GQ, 128], F32) for nm in ('Hre', 'Him', 'Fre', 'Fim', 'Wre', 'Wim')}
        Macc = Q.tile("Macc", [128, 64, 128], F32)
        tmpM = Q.tile("tmpM", [128, 128], F32)
        Wst = [Q.tile("Wst", [128, GQ, 2, 64], BF16) for _ in range(2)]
        Fst = [Q.tile("Fst", [64, GQ, 2, 128], BF16) for _ in range(2)]
        Mst = [Q.tile("Mst", [128, 32, 128], BF16) for _ in range(2)]
        t1 = Q.tile("t1", [64, GQ, 16], F32); t2 = Q.tile("t2", [64, GQ, 16], F32)
        dv = lambda fn, r, w: S.add('dve', fn, r=r, w=w)
        qi = 0
        for d_ in range(2):
            for gq in range(4):
                g0 = d_ * 64 + gq * GQ
                sl = slice(g0, g0 + GQ)
                bc = lambda a, k, sl=sl: a[:, k, sl].unsqueeze(2).to_broadcast([64, GQ, 16])

                def cmul(ore, oim, xre, xim, sre, sim, negim, keys_w, tag):
                    dv(lambda e: e.tensor_tensor(out=t1[:], in0=xre, in1=sre, op=ALU.mult), ['B', 'C', 'apw'], ['t1'])
                    dv(lambda e: e.tensor_tensor(out=t2[:], in0=xim, in1=sim, op=ALU.mult), ['B', 'C', 'apw'], ['t2'])
                    dv(lambda e: e.tensor_tensor(out=ore, in0=t1[:], in1=t2[:], op=ALU.subtract), ['t1', 't2'], [keys_w[0]])
                    dv(lambda e: e.tensor_tensor(out=t1[:], in0=xre, in1=sim, op=ALU.mult), ['B', 'C', 'apw'], ['t1'])
                    dv(lambda e: e.tensor_tensor(out=t2[:], in0=xim, in1=sre, op=ALU.mult), ['B', 'C', 'apw'], ['t2'])
                    if negim:
                        dv(lambda e: e.scalar_tensor_tensor(out=oim, in0=t1[:], scalar=-1.0, in1=t2[:], op0=ALU.mult, op1=ALU.subtract), ['t1', 't2'], [keys_w[1]])
                    else:
                        dv(lambda e: e.tensor_tensor(out=oim, in0=t1[:], in1=t2[:], op=ALU.add), ['t1', 't2'], [keys_w[1]])

                for sg in range(8):
                    kH = sg + 1 if d_ == 0 else 8 - sg
                    kW = 7 - sg if d_ == 0 else sg
                    kF = sg + 1 if d_ == 0 else 8 - sg
                    cs = slice(sg * 16, (sg + 1) * 16)
                    cmul(tabs['Hre'][:, :, cs], tabs['Him'][:, :, cs], Bre[:, sl, :], Bim[:, sl, :], bc(ai_re, kH), bc(ai_im, kH), False, ['Hre', 'Him'], 'H')
                    cmul(tabs['Wre'][:, :, cs], tabs['Wim'][:, :, cs], Bre[:, sl, :], Bim[:, sl, :], bc(ap_re, kW), bc(ap_im, kW), False, ['Wre', 'Wim'], 'W')
                    cmul(tabs['Fre'][:, :, cs], tabs['Fim'][:, :, cs], CTre[:, sl, :], CTim[:, sl, :], bc(ap_re, kF), bc(ap_im, kF), True, ['Fre', 'Fim'], 'F')
                wi = qi % 2
                qi += 1
                for gl in range(GQ):
                    g = gq * GQ + gl
                    bk = gl % 2
                    S.add('pe', lambda e, gl=gl, bk=bk: e.matmul(self.ps[bk][:, 0:128], lhsT=tabs['Hre'][:, gl, :], rhs=tabs['Fre'][:, gl, :], start=True, stop=False), r=['Hre', 'Fre'], w=[('ps', bk)])
                    S.add('pe', lambda e, gl=gl, bk=bk: e.matmul(self.ps[bk][:, 0:128], lhsT=tabs['Him'][:, gl, :], rhs=tabs['Fim'][:, gl, :], start=False, stop=True), r=['Him', 'Fim'], w=[('ps', bk)])
                    if d_ == 0:
                        dv(lambda e, g=g, bk=bk: e.tensor_tensor(out=Macc[:, g, :], in0=self.ps[bk][:, 0:128], in1=maskF[:], op=ALU.mult), [('ps', bk)], [('Macc', g)])
                    else:
                        dv(lambda e, g=g, bk=bk: e.tensor_tensor(out=tmpM[:], in0=self.ps[bk][:, 0:128], in1=maskB[:], op=ALU.mult), [('ps', bk)], ['tmpM'])
                        dv(lambda e, g=g: e.tensor_tensor(out=Macc[:, g, :], in0=Macc[:, g, :], in1=tmpM[:], op=ALU.add), [('Macc', g), 'tmpM'], [('Macc', g)])
                    bk2 = 2 + gl % 2
                    S.add('pe', lambda e, gl=gl, bk2=bk2: e.transpose(self.ps[bk2][:, 0:64], tabs['Wre'][:, gl, :], self.ident[0:64, 0:64]), r=['Wre'], w=[('ps', bk2, 0)])
                    S.add('pe', lambda e, gl=gl, bk2=bk2: e.transpose(self.ps[bk2][:, 64:128], tabs['Wim'][:, gl, :], self.ident[0:64, 0:64]), r=['Wim'], w=[('ps', bk2, 1)])
                    S.add('act', lambda e, gl=gl, bk2=bk2, wi=wi: e.copy(out=Wst[wi][:, gl, :, :].rearrange("p a b -> p (a b)"), in_=self.ps[bk2][:, 0:128]),
                          r=[('ps', bk2, 0), ('ps', bk2, 1)], w=[('Wst', wi)])
                S.add('act', lambda e, wi=wi: e.copy(out=Fst[wi][:, :, 0, :], in_=tabs['Fre'][:]), r=['Fre'], w=[('Fst', wi, 0)])
                S.add('act', lambda e, wi=wi: e.copy(out=Fst[wi][:, :, 1, :], in_=tabs['Fim'][:]), r=['Fim'], w=[('Fst', wi, 1)])
                S.add('sp', lambda e, wi=wi, d_=d_, gq=gq: e.dma_start(out=self.Wtab.ap()[:, d_, gq * GQ:(gq + 1) * GQ], in_=Wst[wi][:]), r=[('Wst', wi)], w=[('Wtab', d_, gq)], dma=True)
                S.add('sp', lambda e, wi=wi, d_=d_, gq=gq: e.dma_start(out=self.Ftab.ap()[:, d_, gq * GQ:(gq + 1) * GQ], in_=Fst[wi][:]), r=[('Fst', wi, 0), ('Fst', wi, 1)], w=[('Ftab', d_, gq)], dma=True)
        for g in range(64):
            dv(lambda e, g=g: e.scalar_tensor_tensor(out=Macc[:, g, :], in0=self.ident[:], scalar=dcol[:, g:g + 1], in1=Macc[:, g, :], op0=ALU.mult, op1=ALU.add), [('Macc', g)], [('Macc', g)])
        for hh in range(2):
            S.add('act', lambda e, hh=hh: e.copy(out=Mst[hh][:], in_=Macc[:, hh * 32:(hh + 1) * 32, :]), r=[('Macc', g) for g in range(hh * 32, hh * 32 + 32)], w=[('Mst', hh)])
            S.add('sp', lambda e, hh=hh: e.dma_start(out=self.Mtab.ap()[:, hh * 32:(hh + 1) * 32, :], in_=Mst[hh][:]), r=[('Mst', hh)], w=[('Mtab', hh)], dma=True)
        S.barrier()

    def s5_main(self, li):
        nc, S = self.nc, self.S
        j = li // 2
        L = self.L
        NJ = L // 8
        assert NJ == 512 or True
        A = self.top.sub()
        X = A.tile("X", [128, 64, NJ], BF16)
        selA = A.tile("selA", [128, 8, 8, 128], BF16)
        selB = A.tile("selB", [128, 8, 8, 128], BF16)
        KSc = A.tile("KSc", [128, 2, 32, 27], F32)
        wglu = A.tile("wglu", [128, NDT, D], BF16)
        glub = A.tile("glub", [128, NDT], F32)
        S.add('sp', lambda e: e.dma_start(out=selA[:], in_=self.selA_in.ap()), w=['selA'], dma=True)
        S.add('sp', lambda e: e.dma_start(out=selB[:], in_=self.selB_in.ap()), w=['selB'], dma=True)
        for gi in range(2):
            S.add('sp', lambda e, gi=gi: e.dma_start(out=KSc[gi * 64:(gi + 1) * 64], in_=self.KSd.ap()[:, :, :, gi, :]), w=[('KSc', gi)], dma=True)
        wsrc = self.wglu_s[j].ap().rearrange("(dt p) n -> dt p n", p=128)
        for dt in range(NDT):
            S.add('sp', lambda e, dt=dt: e.dma_start(out=wglu[:, dt, :], in_=wsrc[dt]), r=[('wglu_s', j, dt)], w=[('wglu', dt)], dma=True)
        S.add('sp', lambda e: e.dma_start(out=glub[:], in_=self.s5_glu_b.ap()[j].rearrange("(dt p) -> p dt", p=128), allow_slow_non_contiguous=True), w=['glub'], dma=True)
        S.barrier()
        for s_ in range(self.NSEQ):
            self.s5_A(li, s_, A, X, selA)
            self.s5_B(li, s_, A, X, KSc)
            self.s5_C(li, s_, A, X, selB, wglu, glub)

    def s5_A(self, li, s_, A0, X, selA):
        nc, S = self.nc, self.S
        L = self.L
        A = A0.sub()
        hT = [A.tile("hT", [128, NDT, 512], F32) for _ in range(2)]
        hn = A.tile("hn", [128, NDT, 512], BF16)
        sq = A.tile("sq", [128, NDT, 512], BF16)
        rtmp = A.tile("rtmp", [128, 512], F32)
        rstd = A.tile("rstd", [128, 512], F32)
        eps_t = A.tile("eps", [128, 1], F32)
        S.add('dve', lambda e: e.memset(eps_t[:], NORM_EPS), w=['eps'])
        hsrc = self.hcur.ap().rearrange("(dt p) t -> p dt t", p=128)
        base = s_ * L
        ev = 0
        for b in range(L // 512):
            i = b % 2
            hkeys = [('hT', i, dt) for dt in range(NDT)]
            S.add('sp', lambda e, i=i, b=b: e.dma_start(out=hT[i][:], in_=hsrc[:, :, base + b * 512: base + (b + 1) * 512]), w=hkeys, dma=True)
            self.rmsnorm_fm(hT[i], hkeys, 512, li, sq, [('sq', dt) for dt in range(NDT)], self.ps[0], ('ps', 0), rtmp, rstd,
                            (lambda dt: hn[:, dt, :]), [('hn', dt) for dt in range(NDT)], eps_t)
            for dt in range(NDT):
                bk = 1 + dt % 4
                for gs in range(8):
                    for tau in range(8):
                        S.add('pe', lambda e, dt=dt, gs=gs, tau=tau, bk=bk: e.matmul(self.ps[bk][:, gs * 64:(gs + 1) * 64], lhsT=selA[:, tau, gs, :],
                                                                                   rhs=hn[:, dt, :].rearrange("p (j t) -> p t j", t=8)[:, tau, :], start=(tau == 0), stop=(tau == 7)),
                              r=['selA', ('hn', dt)], w=[('ps', bk, gs)])
                ev += 1
                eng = 'act' if ev % 2 == 0 else 'dve'
                if eng == 'act':
                    S.add('act', lambda e, dt=dt, b=b, bk=bk: e.copy(out=X[:, dt * 8:(dt + 1) * 8, b * 64:(b + 1) * 64], in_=self.ps[bk].rearrange("p (g j) -> p g j", g=8)),
                          r=[('ps', bk, gs) for gs in range(8)], w=[('X', dt * 8 + gs) for gs in range(8)])
                else:
                    S.add('dve', lambda e, dt=dt, b=b, bk=bk: e.tensor_copy(out=X[:, dt * 8:(dt + 1) * 8, b * 64:(b + 1) * 64], in_=self.ps[bk].rearrange("p (g j) -> p g j", g=8)),
                          r=[('ps', bk, gs) for gs in range(8)], w=[('X', dt * 8 + gs) for gs in range(8)])
        S.barrier()

    def s5_B(self, li, s_, A0, X, KSc):
        nc, S = self.nc, self.S
        L = self.L
        NJ = L // 8
        PAD = NJ // 2
        A = A0.sub()
        KB = [[[A.tile("KB", [128, PAD + NJ], F32) for _ in range(2)] for _ in range(2)] for _ in range(2)]
        Sbf = [[A.tile("Sbf", [128, NJ + 1], BF16) for _ in range(2)] for _ in range(2)]
        Wu = [A.tile("Wu", [128, 2, 2, 2, 64], BF16) for _ in range(2)]
        Fu = [A.tile("Fu", [128, 2, 2, 128], BF16) for _ in range(2)]
        Mu = [A.tile("Mu", [128, 2, 128], BF16) for _ in range(2)]
        for d_ in range(2):
            for pp in range(2):
                for part in range(2):
                    S.add('dve', lambda e, d_=d_, pp=pp, part=part: e.memset(KB[d_][pp][part][:], 0.0), w=[('KB', d_, pp, part)])
            for part in range(2):
                S.add('dve', lambda e, d_=d_, part=part: e.memset(Sbf[d_][part][:], 0.0), w=[('Sbf', d_, part)])
        nsteps = int(math.log2(NJ))
        for u in range(32):
            i = u % 2
            S.add('sp', lambda e, i=i, u=u: e.dma_start(out=Wu[i][:], in_=self.Wtab.ap()[:, :, 2 * u:2 * u + 2]), r=[('Wtab', d_, gq) for d_ in range(2) for gq in range(4)], w=[('Wu', i)], dma=True)
            for gi in range(2):
                S.add('sp', lambda e, i=i, u=u, gi=gi: e.dma_start(out=Fu[i][gi * 64:(gi + 1) * 64], in_=self.Ftab.ap()[:, :, 2 * u + gi]), r=[('Ftab', d_, gq) for d_ in range(2) for gq in range(4)], w=[('Fu', i, gi)], dma=True)
            S.add('sp', lambda e, i=i, u=u: e.dma_start(out=Mu[i][:], in_=self.Mtab.ap()[:, 2 * u:2 * u + 2, :]), r=[('Mtab', 0), ('Mtab', 1)], w=[('Mu', i)], dma=True)
            for d_ in range(2):
                for part in range(2):
                    bk = d_ * 2 + part
                    for gi in range(2):
                        g = 2 * u + gi
                        if gi == 0:
                            S.add('pe', lambda e, i=i, d_=d_, part=part, g=g, bk=bk: e.matmul(self.ps[bk][0:64, 0:NJ], lhsT=Wu[i][:, d_, 0, part, :], rhs=X[:, g, :], start=True, stop=True),
                                  r=[('Wu', i), ('X', g)], w=[('ps', bk, 0)])
                        else:
                            S.add('pe', lambda e, i=i, d_=d_, part=part, g=g, bk=bk: e.matmul(self.ps[bk][64:128, 0:NJ], lhsT=Wu[i][:, d_, 1, part, :], rhs=X[:, g, :], start=True, stop=True, tile_position=(0, 64)),
                                  r=[('Wu', i), ('X', g)], w=[('ps', bk, 1)])
                    off = PAD if d_ == 0 else 0
                    S.add('act', lambda e, d_=d_, part=part, bk=bk, off=off: e.copy(out=KB[d_][0][part][:, off:off + NJ], in_=self.ps[bk][:, 0:NJ]),
                          r=[('ps', bk, 0), ('ps', bk, 1)], w=[('KB', d_, 0, part)])
            for k in range(nsteps):
                last = (k == nsteps - 1)
                stage1, stage2 = [], []
                for d_ in range(2):
                    off = PAD if d_ == 0 else 0
                    sft = (1 << k) if d_ == 0 else -(1 << k)
                    src = KB[d_][k % 2]
                    sre, sim = src[0], src[1]
                    cur = slice(off, off + NJ)
                    shf = slice(off - sft, off - sft + NJ)
                    sc = lambda c, u=u, d_=d_, k=k: KSc[:, d_, u, 3 * k + c:3 * k + c + 1]
                    rk = [('KB', d_, k % 2, 0), ('KB', d_, k % 2, 1), ('KSc', 0), ('KSc', 1)]
                    tre = KB[d_][(k + 1) % 2][0][:, off:off + NJ]
                    tim = KB[d_][(k + 1) % 2][1][:, off:off + NJ]
                    tkr, tki = ('KB', d_, (k + 1) % 2, 0), ('KB', d_, (k + 1) % 2, 1)
                    if last:
                        so = 1 if d_ == 0 else 0
                        dre = Sbf[d_][0][:, so:so + NJ]
                        dim = Sbf[d_][1][:, so:so + NJ]
                        kre, kim = ('Sbf', d_, 0), ('Sbf', d_, 1)
                    else:
                        dre, dim, kre, kim = tre, tim, tkr, tki
                    stage1.append((lambda e, sim=sim, sre=sre, shf=shf, cur=cur, tre=tre, sc=sc: e.scalar_tensor_tensor(out=tre, in0=sim[:, shf], scalar=sc(2), in1=sre[:, cur], op0=ALU.mult, op1=ALU.add), rk, [tkr]))
                    stage1.append((lambda e, sim=sim, sre=sre, shf=shf, cur=cur, tim=tim, sc=sc: e.scalar_tensor_tensor(out=tim, in0=sre[:, shf], scalar=sc(1), in1=sim[:, cur], op0=ALU.mult, op1=ALU.add), rk, [tki]))
                    stage2.append((lambda e, sre=sre, shf=shf, tre=tre, dre=dre, sc=sc: e.scalar_tensor_tensor(out=dre, in0=sre[:, shf], scalar=sc(0), in1=tre, op0=ALU.mult, op1=ALU.add), rk + [tkr], [kre]))
                    stage2.append((lambda e, sim=sim, shf=shf, tim=tim, dim=dim, sc=sc: e.scalar_tensor_tensor(out=dim, in0=sim[:, shf], scalar=sc(0), in1=tim, op0=ALU.mult, op1=ALU.add), rk + [tki], [kim]))
                for fn, r_, w_ in stage1 + stage2:
                    S.add('dve', fn, r=r_, w=w_)
            for gi in range(2):
                g = 2 * u + gi
                bk = 4 + (2 * u + gi) % 4
                S.add('pe', lambda e, i=i, gi=gi, g=g, bk=bk: e.matmul(self.ps[bk][:, 0:NJ], lhsT=Mu[i][:, gi, :], rhs=X[:, g, :], start=True, stop=False), r=[('Mu', i), ('X', g)], w=[('ps', bk)])
                n = 0
                for d_ in range(2):
                    so = 0 if d_ == 0 else 1
                    for part in range(2):
                        n += 1
                        S.add('pe', lambda e, i=i, gi=gi, d_=d_, part=part, so=so, bk=bk, n=n: e.matmul(self.ps[bk][:, 0:NJ], lhsT=Fu[i][gi * 64:(gi + 1) * 64, d_, part, :],
                                                                                                  rhs=Sbf[d_][part][gi * 64:(gi + 1) * 64, so:so + NJ], start=False, stop=(n == 4)),
                              r=[('Fu', i, gi), ('Sbf', d_, part)], w=[('ps', bk)])
                S.add('act', lambda e, g=g, bk=bk: e.activation(out=X[:, g, :], in_=self.ps[bk][:, 0:NJ], func=AF.Gelu_apprx_tanh), r=[('ps', bk)], w=[('X', g)])
        S.barrier()

    def s5_C(self, li, s_, A0, X, selB, wglu, glub):
        nc, S = self.nc, self.S
        L = self.L
        A = A0.sub()
        hT = [A.tile("hT", [128, NDT, 512], F32) for _ in range(2)]
        gT = A.tile("gT", [128, NDT, 512], BF16)
        sg = [A.tile("sg", [128, 512], F32) for _ in range(2)]
        hsrc = self.hcur.ap().rearrange("(dt p) t -> p dt t", p=128)
        hdst = self.hnxt.ap().rearrange("(dt p) t -> p dt t", p=128)
        base = s_ * L
        ev = 0
        for b in range(L // 512):
            i = b % 2
            hkeys = [('hT', i, dt) for dt in range(NDT)]
            S.add('sp', lambda e, i=i, b=b: e.dma_start(out=hT[i][:], in_=hsrc[:, :, base + b * 512: base + (b + 1) * 512]), w=hkeys, dma=True)
            for dt in range(NDT):
                bk = dt % 4
                for tau in range(8):
                    for gs in range(8):
                        S.add('pe', lambda e, dt=dt, gs=gs, tau=tau, bk=bk, b=b: e.matmul(self.ps[bk].rearrange("p (j t) -> p t j", t=8)[:, tau, :], lhsT=selB[:, tau, gs, :],
                                                                                        rhs=X[:, dt * 8 + gs, b * 64:(b + 1) * 64], start=(gs == 0), stop=(gs == 7)),
                              r=['selB', ('X', dt * 8 + gs)], w=[('ps', bk, tau)])
                ev += 1
                if ev % 2 == 0:
                    S.add('act', lambda e, dt=dt, bk=bk: e.copy(out=gT[:, dt, :], in_=self.ps[bk]), r=[('ps', bk, tau) for tau in range(8)], w=[('gT', dt)])
                else:
                    S.add('dve', lambda e, dt=dt, bk=bk: e.tensor_copy(out=gT[:, dt, :], in_=self.ps[bk]), r=[('ps', bk, tau) for tau in range(8)], w=[('gT', dt)])
            for nt in range(NDT):
                bk = 4 + nt % 4
                k2 = nt % 2
                for dt in range(NDT):
                    S.add('pe', lambda e, nt=nt, dt=dt, bk=bk: e.matmul(self.ps[bk], lhsT=wglu[:, dt, nt * 128:(nt + 1) * 128], rhs=gT[:, dt, :], start=(dt == 0), stop=(dt == NDT - 1)),
                          r=[('wglu', dt), ('gT', dt)], w=[('ps', bk)])
                S.add('act', lambda e, nt=nt, bk=bk, k2=k2: e.activation(out=sg[k2][:], in_=self.ps[bk], func=AF.Sigmoid, bias=glub[:, nt:nt + 1]), r=[('ps', bk), 'glub'], w=[('sg', k2)])
                S.add('dve', lambda e, nt=nt, k2=k2: e.tensor_tensor(out=sg[k2][:], in0=sg[k2][:], in1=gT[:, nt, :], op=ALU.mult), r=[('sg', k2), ('gT', nt)], w=[('sg', k2)])
                S.add('dve', lambda e, nt=nt, k2=k2, i=i: e.tensor_tensor(out=hT[i][:, nt, :], in0=hT[i][:, nt, :], in1=sg[k2][:], op=ALU.add), r=[('sg', k2), ('hT', i, nt)], w=[('hT', i, nt)])
            S.add('sp', lambda e, i=i, b=b: e.dma_start(out=hdst[:, :, base + b * 512: base + (b + 1) * 512], in_=hT[i][:]), r=hkeys, w=[('h3', s_, b)], dma=True)
        S.barrier()

    def phase_out(self):
        nc, S = self.nc, self.S
        depth = self.cfg['depth']
        A = self.top.sub()
        hT = [A.tile("hT", [128, NDT, 512], F32) for _ in range(2)]
        hn = [A.tile("hnf", [128, NDT, 512], F32) for _ in range(2)]
        sq = A.tile("sq", [128, NDT, 512], BF16)
        rtmp = A.tile("rtmp", [128, 512], F32)
        rstd = A.tile("rstd", [128, 512], F32)
        ot = [A.tile("ot", [128, 4, D], F32) for _ in range(2)]
        eps_t = A.tile("eps", [128, 1], F32)
        S.add('dve', lambda e: e.memset(eps_t[:], NORM_EPS), w=['eps'])
        hsrc = self.hcur.ap().rearrange("(dt p) t -> p dt t", p=128)
        ov = self.out.ap().rearrange("(b t p) d -> b p t d", p=128, t=4)
        nblk = self.NTOK // 512
        for b in range(nblk):
            i = b % 2
            hkeys = [('hT', i, dt) for dt in range(NDT)]
            S.add('sp', lambda e, i=i, b=b: e.dma_start(out=hT[i][:], in_=hsrc[:, :, b * 512:(b + 1) * 512]), w=hkeys, dma=True)
            self.rmsnorm_fm(hT[i], hkeys, 512, 2 * depth, sq, [('sq', dt) for dt in range(NDT)], self.ps[0], ('ps', 0), rtmp, rstd,
                            (lambda dt, i=i: hn[i][:, dt, :]), [('hnf', i, dt) for dt in range(NDT)], eps_t)
            for t in range(4):
                for half in range(2):
                    bank = self.ps[1 + (t * 2 + half) % 4]
                    bkey = ('ps', 1 + (t * 2 + half) % 4)
                    for q in range(4):
                        dt = half * 4 + q
                        S.add('pe', lambda e, i=i, dt=dt, t=t, q=q, bank=bank: e.transpose(bank[:, q * 128:(q + 1) * 128], hn[i][:, dt, t * 128:(t + 1) * 128], self.ident[:]),
                              r=[('hnf', i, dt), 'ident'], w=[(bkey, q)])
                    if half == 0:
                        S.add('act', lambda e, i=i, t=t, half=half, bank=bank: e.copy(out=ot[i][:, t, half * 512:(half + 1) * 512], in_=bank[:]),
                              r=[(bkey, q) for q in range(4)], w=[('ot', i, t, half)])
                    else:
                        S.add('dve', lambda e, i=i, t=t, half=half, bank=bank: e.tensor_copy(out=ot[i][:, t, half * 512:(half + 1) * 512], in_=bank[:]),
                              r=[(bkey, q) for q in range(4)], w=[('ot', i, t, half)])
            S.add('sp', lambda e, i=i, b=b: e.dma_start(out=ov[b], in_=ot[i][:]), r=[('ot', i, t, h) for t in range(4) for h in range(2)], w=[('out', b)], dma=True)


def make_cfg(nseq=2, L=4096, depth=4, layers=None):
    if layers is None:
        layers = []
        for i in range(depth):
            layers.append(('s5' if i % 2 == 0 else 'attn', i))
            layers.append(('ffn', i))
    return dict(nseq=nseq, L=L, depth=depth, nA=(depth + 1) // 2, nB=depth // 2, layers=layers,
                ffn_layers=[li for (k, li) in layers if k == 'ffn'])


def _rel_bucket_np(rel):
    nb = 16
    ret = np.where(rel > 0, nb, 0)
    n = np.abs(rel)
    max_exact = nb // 2
    nf = np.maximum(n, 1).astype(np.float32)
    large = max_exact + (np.log(nf / np.float32(max_exact)) / np.float32(math.log(128 / max_exact)) * np.float32(nb - max_exact)).astype(np.int32)
    large = np.minimum(large, nb - 1)
    return ret + np.where(n < max_exact, n, large)


def host_consts(rel_bias=None, s5=True):
    c = {"ident": np.eye(128, dtype=np.float32)}
    if s5:
        selA = np.zeros((128, 8, 8, 128), dtype=np.float32)
        for tau in range(8):
            for gs in range(8):
                for cc in range(16):
                    selA[gs * 16 + cc, tau, gs, tau * 16 + cc] = 1.0
        selB = np.ascontiguousarray(selA.transpose(3, 1, 2, 0))
        c["selA"] = selA.astype(ml_dtypes.bfloat16)
        c["selB"] = selB.astype(ml_dtypes.bfloat16)
        sig = np.arange(128)[:, None] // 16
        tau = np.arange(128)[None, :] // 16
        c["maskF"] = (sig <= tau).astype(np.float32)
        c["maskB"] = (sig >= tau).astype(np.float32)
    if rel_bias is not None:
        rb = np.asarray(rel_bias, dtype=np.float32)
        p = np.arange(128)[:, None]
        xx = np.arange(1152)[None, :]
        idx = _rel_bucket_np(p - xx + 512)
        c["reltab"] = np.ascontiguousarray(rb[idx].transpose(2, 0, 1))
        far = np.stack([rb[15], rb[31]], axis=1).reshape(1, 16)
        c["relfar"] = np.ascontiguousarray(np.broadcast_to(far, (128, 16)))
    return c


_CACHE = {}


def kernel(**inputs):
    ncores = 8
    cfg = make_cfg()
    if 'nc' not in _CACHE:
        _CACHE['nc'] = K(cfg).build()
    nc = _CACHE['nc']
    x = np.ascontiguousarray(inputs['x'], dtype=np.float32)
    B, L, _ = x.shape
    per = B // ncores
    consts = host_consts(inputs['rel_bias'], s5=True)
    shared = {}
    for k, v in inputs.items():
        if k in ('x', 'rel_bias'):
            continue
        shared[k] = np.ascontiguousarray(v, dtype=np.float32)
    in_maps = []
    for c in range(ncores):
        m = {"x": x[c * per:(c + 1) * per].reshape(per * L, D)}
        m.update(shared)
        m.update(consts)
        in_maps.append(m)
    res = run_bass_kernel_spmd(nc, in_maps, core_ids=list(range(ncores)))
    outs = [r["out"].reshape(per, L, D) for r in res.results]
    return np.concatenate(outs, axis=0)
```
